# Optimizing a Trainium2 kernel written in Bass

```python
import math
import jax, jax.numpy as jnp
from jax import lax
import numpy as np

D_MODEL = 2048
BATCH = 1
SEQ = 16384
DEPTH = 2
DEC_BATCH = 4
DEC_SEQ = 8192
PAST_LEN = 128

GRID_W = 64
N_POOL_GROUPS = 4
POOL_WINDOWS = (2, 4, 8, 16)
GROUP_DIM = D_MODEL // N_POOL_GROUPS
N_HEADS = 16
HEAD_DIM = D_MODEL // N_HEADS
WIN_H = 8
WIN_W = 16
D_FF = 5632
CONV_W = 3
N_MIXERS = 2
N_POOL_LAYERS = (DEPTH + 1) // 2
N_ATTN_LAYERS = DEPTH // 2
EPS = 1e-6
NEG_INF = -1e30

kernel_name = "hybrid_pool_natten_convffn_encoder"


def rmsnorm(x, g):
    xf = x.astype(jnp.float32)
    y = xf * lax.rsqrt(jnp.mean(xf * xf, axis=-1, keepdims=True) + EPS)
    return (y * g.astype(jnp.float32)).astype(x.dtype)


def multiscale_pool_mixer(u, w, b, scale):
    B, S, _ = u.shape
    ug = u.reshape(B, S, N_POOL_GROUPS, GROUP_DIM)
    t = np.arange(S)
    outs = []
    for g, wsz in enumerate(POOL_WINDOWS):
        xg = ug[:, :, g, :]
        xf = xg.astype(jnp.float32)
        cs = jnp.concatenate([jnp.zeros((B, 1, GROUP_DIM), jnp.float32),
                              jnp.cumsum(xf, axis=1)], axis=1)
        lo = np.clip(t - wsz // 2, 0, S)
        hi = np.clip(t - wsz // 2 + wsz, 0, S)
        cnt = jnp.asarray((hi - lo).astype(np.float32))[None, :, None]
        mean = (jnp.take(cs, jnp.asarray(hi), axis=1) - jnp.take(cs, jnp.asarray(lo), axis=1)) / cnt
        d = (mean - xf).astype(u.dtype)
        outs.append(d @ w[g] + b[g])
    return jnp.concatenate(outs, axis=-1) * scale


def neighbourhood_attention(u, w_qkv, b_qkv, rpb, w_o):
    B, S, _ = u.shape
    rows = S // GRID_W
    kh = min(WIN_H, rows)
    qkv = u @ w_qkv + b_qkv
    q, k, v = jnp.split(qkv, 3, axis=-1)
    q = q.reshape(B, rows, GRID_W, N_HEADS, HEAD_DIM) * (HEAD_DIM ** -0.5)
    k = k.reshape(B, rows, GRID_W, N_HEADS, HEAD_DIM)
    v = v.reshape(B, rows, GRID_W, N_HEADS, HEAD_DIM)

    c = np.arange(GRID_W)
    cstart = np.clip(c - WIN_W // 2, 0, GRID_W - WIN_W)
    col_mask = jnp.asarray((c[None, :] >= cstart[:, None]) & (c[None, :] < cstart[:, None] + WIN_W))
    col_idx = jnp.asarray(np.clip(c[None, :] - c[:, None], -(WIN_W - 1), WIN_W - 1) + WIN_W - 1)

    def row_step(r):
        rs = jnp.clip(r - kh // 2, 0, rows - kh)
        k_band = lax.dynamic_slice_in_dim(k, rs, kh, axis=1)
        v_band = lax.dynamic_slice_in_dim(v, rs, kh, axis=1)
        q_row = lax.dynamic_index_in_dim(q, r, axis=1, keepdims=False)
        s = jnp.einsum('bchd,brkhd->bhcrk', q_row, k_band).astype(jnp.float32)
        row_idx = rs + jnp.arange(kh) - r + (WIN_H - 1)
        bias = rpb[:, row_idx[None, :, None], col_idx[:, None, :]]
        s = s + bias.astype(jnp.float32)[None]
        s = jnp.where(col_mask[None, None, :, None, :], s, NEG_INF)
        p = jax.nn.softmax(s.reshape(B, N_HEADS, GRID_W, kh * GRID_W), axis=-1)
        p = p.reshape(B, N_HEADS, GRID_W, kh, GRID_W).astype(v.dtype)
        return jnp.einsum('bhcrk,brkhd->bchd', p, v_band)

    o = lax.map(row_step, jnp.arange(rows))
    o = jnp.moveaxis(o, 0, 1).reshape(B, S, D_MODEL)
    return o @ w_o


def conv_ffn(u, w_up, conv_w, conv_b, w_down):
    h = u @ w_up
    gate, val = jnp.split(h, 2, axis=-1)
    gp = jnp.pad(gate, ((0, 0), (1, 1), (0, 0)))
    gate = gp[:, :-2] * conv_w[0] + gp[:, 1:-1] * conv_w[1] + gp[:, 2:] * conv_w[2] + conv_b
    return (jax.nn.gelu(gate, approximate=False) * val) @ w_down


def trunk(x, mix_norm, pool_w, pool_b, pool_scale, attn_w_qkv, attn_b_qkv, attn_rpb, attn_w_o,
          ffn_norm, ffn_w_up, ffn_conv_w, ffn_conv_b, ffn_w_down, final_norm):
    for i in range(DEPTH):
        u = rmsnorm(x, mix_norm[i])
        j = i // N_MIXERS
        if i % N_MIXERS == 0:
            x = x + multiscale_pool_mixer(u, pool_w[j], pool_b[j], pool_scale[j])
        else:
            x = x + neighbourhood_attention(u, attn_w_qkv[j], attn_b_qkv[j], attn_rpb[j], attn_w_o[j])
        u = rmsnorm(x, ffn_norm[i])
        x = x + conv_ffn(u, ffn_w_up[i], ffn_conv_w[i], ffn_conv_b[i], ffn_w_down[i])
    return rmsnorm(x, final_norm)


def setup_inputs(seed: int = 0) -> dict:
    key = jax.random.key(seed)
    ks = jax.random.split(key, 20)
    f32 = jnp.float32
    n = lambda k, s, sc: jax.random.normal(k, s, f32) * sc
    return {
        "x_prompt": n(ks[0], (BATCH, SEQ, D_MODEL), 1.0),
        "x_sample": n(ks[1], (DEC_BATCH, DEC_SEQ, D_MODEL), 1.0),
        "mix_norm": 1.0 + n(ks[2], (DEPTH, D_MODEL), 0.02),
        "pool_w": n(ks[3], (N_POOL_LAYERS, N_POOL_GROUPS, GROUP_DIM, GROUP_DIM), GROUP_DIM ** -0.5),
        "pool_b": n(ks[4], (N_POOL_LAYERS, N_POOL_GROUPS, GROUP_DIM), 0.02),
        "pool_scale": 1.0 + n(ks[5], (N_POOL_LAYERS, D_MODEL), 0.02),
        "attn_w_qkv": n(ks[6], (N_ATTN_LAYERS, D_MODEL, 3 * D_MODEL), D_MODEL ** -0.5),
        "attn_b_qkv": n(ks[7], (N_ATTN_LAYERS, 3 * D_MODEL), 0.02),
        "attn_rpb": n(ks[8], (N_ATTN_LAYERS, N_HEADS, 2 * WIN_H - 1, 2 * WIN_W - 1), 0.1),
        "attn_w_o": n(ks[9], (N_ATTN_LAYERS, D_MODEL, D_MODEL), D_MODEL ** -0.5),
        "ffn_norm": 1.0 + n(ks[10], (DEPTH, D_MODEL), 0.02),
        "ffn_w_up": n(ks[11], (DEPTH, D_MODEL, 2 * D_FF), D_MODEL ** -0.5),
        "ffn_conv_w": n(ks[12], (DEPTH, CONV_W, D_FF), CONV_W ** -0.5),
        "ffn_conv_b": n(ks[13], (DEPTH, D_FF), 0.02),
        "ffn_w_down": n(ks[14], (DEPTH, D_FF, D_MODEL), D_FF ** -0.5),
        "final_norm": 1.0 + n(ks[15], (D_MODEL,), 0.02),
    }


def reference(x_prompt, x_sample, mix_norm, pool_w, pool_b, pool_scale, attn_w_qkv, attn_b_qkv,
              attn_rpb, attn_w_o, ffn_norm, ffn_w_up, ffn_conv_w, ffn_conv_b, ffn_w_down, final_norm):
    y_prompt = trunk(x_prompt, mix_norm, pool_w, pool_b, pool_scale, attn_w_qkv, attn_b_qkv,
                     attn_rpb, attn_w_o, ffn_norm, ffn_w_up, ffn_conv_w, ffn_conv_b, ffn_w_down,
                     final_norm)
    y_sample = trunk(x_sample, mix_norm, pool_w, pool_b, pool_scale, attn_w_qkv, attn_b_qkv,
                     attn_rpb, attn_w_o, ffn_norm, ffn_w_up, ffn_conv_w, ffn_conv_b, ffn_w_down,
                     final_norm)
    return (y_prompt, y_sample)
```

```python
import numpy as np
import ml_dtypes
from contextlib import ExitStack
import concourse.bass as bass
import concourse.mybir as mybir
from concourse.bass_utils import run_bass_kernel_spmd

F32 = mybir.dt.float32
BF16 = mybir.dt.bfloat16
AF = mybir.ActivationFunctionType
ALU = mybir.AluOpType

D = 2048
NDC = 16
DFF = 5632
NFC = 44
NFG = 4
FGS = 11
NH = 16
DH = 128
GW = 64
EPS = 1e-6
N_CORES = 8
ROWS_CORE = 96
HALO_TOP = 6
HALO_BOT = 4
NLR = HALO_TOP + ROWS_CORE + HALO_BOT
NLT = NLR * GW
XH = 10
SEQS = [(0, 256), (256, 384), (384, 512), (512, 640), (640, 768)]
TOT_ROWS = 768
POOL_W = (2, 4, 8, 16)

P1_TILES = [(0, HALO_TOP)] + [(HALO_TOP + 8 * k, 8) for k in range(12)] + [(HALO_TOP + 96, HALO_BOT)]
P1_EXT = [r * GW + 2 * XH for (_, r) in P1_TILES]
P1_OFF = [int(x) for x in np.cumsum([0] + P1_EXT[:-1])]
N_EXT = int(sum(P1_EXT))
N_P2 = 12
MAXT = 512
MAXW = MAXT + 2 * XH

O_MIXN, O_FFNN, O_FIN, O_PB, O_PS, O_CW, O_CB, O_BQ, O_BK, O_BV, O_PBS, O_BQS, NPP = \
    0, 32, 64, 80, 96, 112, 376, 464, 480, 496, 512, 528, 544

ENGS = ("pe", "act", "dve", "pool", "sp")
N_DMA_SEMS = 34
DMA_SHARE = {"sp": list(range(0, 16)), "pool": list(range(16, 26)), "act": list(range(26, 34)), "dve": [], "pe": []}


class Buf:
    __slots__ = ("name", "w", "r")

    def __init__(self, name=""):
        self.name = name
        self.w = None
        self.r = []


class Op:
    __slots__ = ("eng", "fn", "deps", "sig", "tok", "dma", "barrier")

    def __init__(self, eng, fn, dma):
        self.eng = eng
        self.fn = fn
        self.deps = []
        self.sig = False
        self.tok = None
        self.dma = dma
        self.barrier = None


class Prog:
    def __init__(self, nc):
        self.nc = nc
        self.ops = {e: [] for e in ENGS}
        self.nbar = 0

    def op(self, eng, fn, reads=(), writes=(), dma=False):
        o = Op(eng, fn, dma)
        deps = {}
        for b in reads:
            if b.w is not None:
                deps[id(b.w)] = b.w
        for b in writes:
            if b.w is not None:
                deps[id(b.w)] = b.w
            for rr in b.r:
                deps[id(rr)] = rr
        for d in deps.values():
            if d.eng == "pe" and eng == "pe" and not d.dma and not dma:
                continue
            d.sig = True
            o.deps.append(d)
        for b in reads:
            b.r.append(o)
        for b in writes:
            b.w = o
            b.r = []
        self.ops[eng].append(o)
        return o

    def barrier(self):
        k = self.nbar
        self.nbar += 1
        for e in ENGS:
            for o in reversed(self.ops[e]):
                if o.barrier is None and not o.dma:
                    o.sig = True
                    break
            m = Op(e, None, False)
            m.barrier = k
            self.ops[e].append(m)

    def emit(self, final_wait_ops=()):
        nc = self.nc
        with ExitStack() as es:
            esem = {e: es.enter_context(nc.semaphore("s_" + e)) for e in ENGS}
            dsems = [es.enter_context(nc.semaphore("d%d" % i)) for i in range(N_DMA_SEMS)]
            ecount = {e: 0 for e in ENGS}
            dcount = [0] * N_DMA_SEMS
            dlast = [None] * N_DMA_SEMS
            drr = {e: 0 for e in ENGS}
            prev_on_sem = {}
            snap_e = {}
            snap_d = {}
            for e in ENGS:
                for o in self.ops[e]:
                    if o.barrier is not None:
                        snap_e.setdefault(o.barrier, {})[e] = ecount[e]
                        sd = snap_d.setdefault(o.barrier, [0] * N_DMA_SEMS)
                        for k in DMA_SHARE[e]:
                            sd[k] = dcount[k]
                    elif o.dma:
                        lst = DMA_SHARE[e]
                        k = lst[drr[e] % len(lst)]
                        drr[e] += 1
                        if dlast[k] is not None:
                            prev_on_sem[id(o)] = dlast[k]
                        dcount[k] += 16
                        o.tok = (k, dcount[k])
                        dlast[k] = o
                    elif o.sig:
                        ecount[e] += 1
                        o.tok = (e, ecount[e])

            def semof(key):
                return esem[key] if isinstance(key, str) else dsems[key]

            block = es.enter_context(nc.Block())
            prog = self

            def run(e, eng):
                known = {}
                for o in prog.ops[e]:
                    if o.barrier is not None:
                        for f, v in snap_e[o.barrier].items():
                            if v > known.get(f, 0):
                                eng.wait_ge(esem[f], v)
                                known[f] = v
                        for k, v in enumerate(snap_d[o.barrier]):
                            if v > known.get(k, 0):
                                eng.wait_ge(dsems[k], v)
                                known[k] = v
                        continue
                    waits = {}
                    dl = o.deps
                    if o.dma and id(o) in prev_on_sem:
                        dl = dl + [prev_on_sem[id(o)]]
                    for d in dl:
                        key, v = d.tok
                        if known.get(key, 0) >= v:
                            continue
                        if waits.get(key, 0) < v:
                            waits[key] = v
                    for key, v in waits.items():
                        eng.wait_ge(semof(key), v)
                        known[key] = v
                    ins = o.fn(eng)
                    if o.tok is not None:
                        key, v = o.tok
                        ins.then_inc(semof(key), 16 if o.dma else 1)
                if e == "sp":
                    for o in final_wait_ops:
                        key, v = o.tok
                        eng.wait_ge(semof(key), v)

            @block.tensor
            def _(eng):
                run("pe", eng)

            @block.scalar
            def _(eng):
                run("act", eng)

            @block.vector
            def _(eng):
                run("dve", eng)

            @block.gpsimd
            def _(eng):
                run("pool", eng)

            @block.sync
            def _(eng):
                run("sp", eng)


def drain(g, n=None):
    k = 0
    while n is None or k < n:
        try:
            next(g)
        except StopIteration:
            return False
        k += 1
    return True


def inblocks(n):
    out, s_ = [], 0
    while s_ < n:
        w = min(128, n - s_)
        out.append((s_, w))
        s_ += w
    return out


def colblocks(n, maxw=512):
    k = -(-n // maxw)
    base, rem = n // k, n % k
    out, s = [], 0
    for i in range(k):
        w = base + (1 if i < rem else 0)
        out.append((s, w))
        s += w
    return out


class Ring:
    def __init__(self, items):
        self.items = items
        self.i = 0

    def next(self):
        it = self.items[self.i % len(self.items)]
        self.i += 1
        return it


def build_program(debug=False, p1_sel=None, p2_sel=None):
    nc = bass.Bass("TRN2", target_bir_lowering=False)
    dt_in = lambda name, shape, dt=F32: nc.dram_tensor(name, shape, dt, kind="ExternalInput")
    skind = dict(kind="ExternalOutput") if debug else {}
    xin = dt_in("xin", [N_EXT, D]).ap()
    vldin = dt_in("vldin", [128, N_EXT]).ap()
    evin = dt_in("evin", [128, 2 * N_P2]).ap()
    mkin = dt_in("mkin", [128, 12 * 12]).ap()
    ppin = dt_in("ppin", [128, NPP]).ap()
    rpbF_h = dt_in("rpbF", [NH * 15 * 64 * 32])
    cmin = dt_in("cmin", [128, 128]).ap()
    idin = dt_in("idin", [128, 128]).ap()
    w_pool = dt_in("w_pool", [4, 512, 512]).ap()
    w_qkv = dt_in("w_qkv", [D, 3 * D]).ap()
    w_o = dt_in("w_o", [D, D]).ap()
    w_up = dt_in("w_up", [2, D, 2 * DFF]).ap()
    w_dn = dt_in("w_dn", [2, DFF, D]).ap()
    yout = nc.dram_tensor("yout", [ROWS_CORE * GW, D], F32, kind="ExternalOutput").ap()
    s_wpool = nc.dram_tensor("s_wpool", [16, 128, 4, 128], BF16).ap()
    s_wup = nc.dram_tensor("s_wup", [2, 88, 128, 16, 128], BF16).ap()
    s_wdn = nc.dram_tensor("s_wdn", [2, 16, NFG, 128, FGS, 128], BF16).ap()
    s_wqkv = nc.dram_tensor("s_wqkv", [48, 128, 16, 128], BF16).ap()
    s_wo = nc.dram_tensor("s_wo", [16, 128, 16, 128], BF16).ap()
    s_x1T = nc.dram_tensor("s_x1T", [D, NLT], F32, **skind).ap()
    s_QT = nc.dram_tensor("s_QT", [D, NLT], BF16, **skind).ap()
    s_KT = nc.dram_tensor("s_KT", [D, NLT], BF16, **skind).ap()
    s_V = nc.dram_tensor("s_V", [NLT, D], BF16, **skind).ap()
    s_E = nc.dram_tensor("s_E", [NH, 128, 12 * 128], F32, **skind).ap()
    if debug:
        dbg_x = nc.dram_tensor("dbg_x", [N_P2, 128, NDC, MAXT + 2], F32, kind="ExternalOutput").ap()
        dbg_o = nc.dram_tensor("dbg_o", [N_P2, 128, NDC, MAXT + 2], BF16, kind="ExternalOutput").ap()

    with ExitStack() as es:
        def SB(name, shape, dt):
            return es.enter_context(nc.sbuf_tensor(name, shape, dt))

        P = Prog(nc)
        X = SB("X", [128, NDC, MAXW], F32)
        U = SB("U", [128, NDC, MAXW], BF16)
        A = SB("A", [128, FGS, MAXT], BF16)
        NWS = 7
        wsl = [SB("wsl%d" % i, [128, 16, 128], BF16) for i in range(NWS)]
        tin = [SB("tin%d" % i, [128, D], F32) for i in range(2)]
        gsb = [SB("gsb%d" % i, [128, MAXT + 2], F32) for i in range(2)]
        cb_ = [SB("cb%d" % i, [128, MAXT], F32) for i in range(2)]
        ge_ = [SB("ge%d" % i, [128, MAXT], F32) for i in range(2)]
        sq_ = [SB("sq%d" % i, [128, 4, MAXW], BF16) for i in range(2)]
        rstd = SB("rstd", [128, MAXW], F32)
        Ue = SB("Ue", [128, NDC, 2], BF16)
        pp = SB("pp", [128, NPP], F32)
        ident = SB("ident", [128, 128], F32)
        identb = SB("identb", [128, 128], BF16)
        ones = SB("ones", [128, 128], BF16)
        epst = SB("epst", [128, 1], F32)
        psT = [es.enter_context(nc.psum_tensor("ps%d" % i, [128, 1024], F32)) for i in range(4)]

        def bank(b):
            return psT[b // 2][:, (b % 2) * 512:(b % 2) * 512 + 512]

        bX = [Buf("X%d" % i) for i in range(NDC)]
        bU = [Buf("U%d" % i) for i in range(NDC)]
        bA = [Buf("A%d" % i) for i in range(FGS)]
        bws = [Buf("ws%d" % i) for i in range(NWS)]
        btin = [Buf() for _ in range(2)]
        bgsb = [Buf() for _ in range(2)]
        bcb = [Buf() for _ in range(2)]
        bge = [Buf() for _ in range(2)]
        bsq = [Buf() for _ in range(2)]
        brstd = Buf("rstd")
        bUe = Buf("Ue")
        bpp = Buf("pp")
        bconst = Buf("const")
        bbank = [Buf("bank%d" % i) for i in range(8)]
        bedge = [Buf("edge%d" % i) for i in range(8)]
        ws_ring = Ring(list(range(NWS)))
        bank_ring = Ring(list(range(7)))
        edge_ring = Ring(list(range(8)))
        r2 = {n: Ring([0, 1]) for n in ("tin", "gsb", "cb", "ge", "sq", "dT", "vT", "vtok", "Etmp", "qk",
                                         "Kb", "Vb", "Eb", "pexp", "PT", "rden", "S", "od")}
        evac_rr = Ring(["act", "dve"])

        edge_loc = [3, 512]

        def edge_ap(k):
            return psT[edge_loc[0]][:, edge_loc[1] + 2 * k: edge_loc[1] + 2 * k + 2]

        P.op("sp", lambda e: e.dma_start(out=pp[:, :], in_=ppin), writes=[bpp], dma=True)
        P.op("sp", lambda e: e.dma_start(out=ident[:, :], in_=idin), writes=[bconst], dma=True)

        P.op("dve", lambda e: e.tensor_copy(identb[:, :], ident[:, :]), reads=[bconst], writes=[bconst])
        P.op("dve", lambda e: e.memset(ones[:, :], 1.0), writes=[bconst])
        P.op("dve", lambda e: e.memset(epst[:, :], EPS), writes=[bconst])
        P.op("dve", lambda e: e.tensor_tensor(out=pp[:, O_PBS:O_PBS + 16], in0=pp[:, O_PB:O_PB + 16],
                                              in1=pp[:, O_PS:O_PS + 16], op=ALU.mult), reads=[bpp], writes=[bpp])
        P.op("dve", lambda e: e.tensor_scalar(out=pp[:, O_BQS:O_BQS + 16], in0=pp[:, O_BQ:O_BQ + 16],
                                              scalar1=float(DH ** -0.5), scalar2=None, op0=ALU.mult), reads=[bpp], writes=[bpp])

        bp_pool = [Buf() for _ in range(16)]
        bp_up = [[Buf() for _ in range(88)] for _ in range(2)]
        bp_dn = [[[Buf() for _ in range(NFG)] for _ in range(16)] for _ in range(2)]
        bp_qkv = [Buf() for _ in range(48)]
        bp_wo = [Buf() for _ in range(16)]

        def cast(dst, src, b):
            P.op("pool", lambda e: e.dma_start(out=dst, in_=src), writes=[b], dma=True)

        def conv_pool_w():
            for g in range(4):
                v = w_pool[g].rearrange("(kc p) (oc m) -> oc p kc m", p=128, m=128)
                for oc in range(4):
                    cast(s_wpool[g * 4 + oc], v[oc], bp_pool[g * 4 + oc])

        def conv_up(l, fcs):
            v = w_up[l].rearrange("(dc p) (fc m) -> fc p dc m", p=128, m=128)
            for fc in fcs:
                cast(s_wup[l, fc], v[fc], bp_up[l][fc])

        def conv_dn(l, fgs):
            v = w_dn[l].rearrange("(fg j p) (oc m) -> oc fg p j m", j=FGS, p=128, m=128)
            for fg in fgs:
                for oc in range(16):
                    cast(s_wdn[l, oc, fg], v[oc, fg], bp_dn[l][oc][fg])

        def conv_qkv():
            v = w_qkv.rearrange("(dc p) (oc m) -> oc p dc m", p=128, m=128)
            for oc in range(48):
                cast(s_wqkv[oc], v[oc], bp_qkv[oc])

        def conv_wo():
            v = w_o.rearrange("(dc p) (oc m) -> oc p dc m", p=128, m=128)
            for oc in range(16):
                cast(s_wo[oc], v[oc], bp_wo[oc])

        conv_pool_w()
        for fg in range(NFG):
            fcs = list(range(fg * FGS, (fg + 1) * FGS))
            conv_up(0, fcs + [NFC + f for f in fcs])
            conv_dn(0, [fg])
        conv_qkv()

        def load_panel(src, b_src, nk=16):
            s = ws_ring.next()
            P.op("sp", lambda e: e.dma_start(out=wsl[s][:, 0:nk, :], in_=src), reads=[b_src], writes=[bws[s]],
                 dma=True)
            return s

        def norm_stats(c0, W):
            blks = colblocks(W)
            bks = [bank_ring.next() for _ in blks]
            for q in range(4):
                s = r2["sq"].next()
                if q % 2 == 0:
                    P.op("dve", lambda e, s=s, q=q: e.tensor_tensor(out=sq_[s][:, :, 0:W], in0=X[:, 4 * q:4 * q + 4, c0:c0 + W],
                                                                    in1=X[:, 4 * q:4 * q + 4, c0:c0 + W], op=ALU.mult),
                         reads=bX[4 * q:4 * q + 4], writes=[bsq[s]])
                else:
                    P.op("act", lambda e, s=s, q=q: e.activation(sq_[s][:, :, 0:W], X[:, 4 * q:4 * q + 4, c0:c0 + W], AF.Square),
                         reads=bX[4 * q:4 * q + 4], writes=[bsq[s]])

                def mm(e, s=s, q=q):
                    r = None
                    for j in range(4):
                        for (b0, bw), bk in zip(blks, bks):
                            r = e.matmul(bank(bk)[:, 0:bw], ones[:, :], sq_[s][:, j, b0:b0 + bw],
                                         start=(q == 0 and j == 0), stop=(q == 3 and j == 3))
                    return r
                P.op("pe", mm, reads=[bsq[s], bconst], writes=[bbank[k] for k in bks])
            for (b0, bw), bk in zip(blks, bks):
                P.op("act", lambda e, b0=b0, bw=bw, bk=bk: e.activation(rstd[:, b0:b0 + bw], bank(bk)[:, 0:bw], AF.Sqrt,
                                                                        bias=epst[:, 0:1], scale=1.0 / D),
                     reads=[bbank[bk], bconst], writes=[brstd])
            P.op("dve", lambda e: e.reciprocal(rstd[:, 0:W], rstd[:, 0:W]), reads=[brstd], writes=[brstd])

        def norm_apply(c0, W, gcol, out_t, oc0, bout):
            for dc in range(NDC):
                P.op("dve", lambda e, dc=dc: e.scalar_tensor_tensor(
                    out=out_t[:, dc, oc0:oc0 + W], in0=X[:, dc, c0:c0 + W], scalar=pp[:, gcol + dc:gcol + dc + 1],
                    in1=rstd[:, 0:W], op0=ALU.mult, op1=ALU.mult),
                    reads=[bX[dc], brstd, bpp], writes=[bout[dc]])

        def ffn(c0, T, l):
            drain(ffn_gen(c0, T, l))

        def ffn_gen(c0, T, l):
            W = T + 2
            norm_stats(c0, W)
            norm_apply(c0, W, O_FFNN + 16 * l, U, 0, bU)
            P.op("dve", lambda e: e.tensor_copy(Ue[:, :, :], U[:, :, 0:W:W - 1]), reads=bU, writes=[bUe])
            cwc = O_CW + l * 132
            cbc = O_CB + l * 44
            for fg in range(NFG):
                for j in range(FGS):
                    fc = fg * FGS + j
                    sg = load_panel(s_wup[l, fc], bp_up[l][fc])
                    sv = load_panel(s_wup[l, NFC + fc], bp_up[l][NFC + fc])
                    bg, bv, ek = bank_ring.next(), bank_ring.next(), edge_ring.next()

                    eap = edge_ap(ek)

                    def mm_g(e, sg=sg, bg=bg, eap=eap):
                        for dc in range(NDC):
                            e.matmul(bank(bg)[:, 0:T], wsl[sg][:, dc, :], U[:, dc, 1:1 + T], start=(dc == 0), stop=(dc == 15))
                        r = None
                        for dc in range(NDC):
                            r = e.matmul(eap, wsl[sg][:, dc, :], Ue[:, dc, :], start=(dc == 0), stop=(dc == 15))
                        return r
                    P.op("pe", mm_g, reads=[bws[sg], bUe] + bU, writes=[bbank[bg], bedge[ek]])

                    def mm_v(e, sv=sv, bv=bv):
                        r = None
                        for dc in range(NDC):
                            r = e.matmul(bank(bv)[:, 0:T], wsl[sv][:, dc, :], U[:, dc, 1:1 + T], start=(dc == 0), stop=(dc == 15))
                        return r
                    P.op("pe", mm_v, reads=[bws[sv]] + bU, writes=[bbank[bv]])
                    g, c, q = r2["gsb"].next(), r2["cb"].next(), r2["ge"].next()

                    def f_copy(e, g=g, bg=bg, eap=eap):
                        e.activation(gsb[g][:, 1:1 + T], bank(bg)[:, 0:T], AF.Identity)
                        return e.activation(gsb[g][:, 0:W:W - 1], eap, AF.Identity)
                    P.op("act", f_copy, reads=[bbank[bg], bedge[ek]], writes=[bgsb[g]])
                    P.op("act", lambda e, c=c, bg=bg, fc=fc: e.activation(
                        cb_[c][:, 0:T], bank(bg)[:, 0:T], AF.Identity, bias=pp[:, cbc + fc:cbc + fc + 1],
                        scale=pp[:, cwc + 44 + fc:cwc + 44 + fc + 1]),
                        reads=[bbank[bg], bpp], writes=[bcb[c]])

                    P.op("dve", lambda e, g=g, c=c, fc=fc: e.scalar_tensor_tensor(
                        out=cb_[c][:, 0:T], in0=gsb[g][:, 0:T], scalar=pp[:, cwc + fc:cwc + fc + 1],
                        in1=cb_[c][:, 0:T], op0=ALU.mult, op1=ALU.add), reads=[bgsb[g], bcb[c], bpp], writes=[bcb[c]])
                    P.op("dve", lambda e, g=g, c=c, fc=fc: e.scalar_tensor_tensor(
                        out=cb_[c][:, 0:T], in0=gsb[g][:, 2:2 + T], scalar=pp[:, cwc + 88 + fc:cwc + 88 + fc + 1],
                        in1=cb_[c][:, 0:T], op0=ALU.mult, op1=ALU.add), reads=[bgsb[g], bcb[c], bpp], writes=[bcb[c]])
                    P.op("act", lambda e, c=c, q=q: e.activation(ge_[q][:, 0:T], cb_[c][:, 0:T], AF.Gelu),
                         reads=[bcb[c]], writes=[bge[q]])
                    P.op("dve", lambda e, q=q, bv=bv, j=j: e.tensor_tensor(out=A[:, j, 0:T], in0=ge_[q][:, 0:T],
                                                                          in1=bank(bv)[:, 0:T], op=ALU.mult),
                         reads=[bge[q], bbank[bv]], writes=[bA[j]])
                    yield
                for oc in range(NDC):
                    sd = load_panel(s_wdn[l, oc, fg], bp_dn[l][oc][fg], nk=FGS)
                    bd = bank_ring.next()

                    def mm_d(e, sd=sd, bd=bd):
                        r = None
                        for j in range(FGS):
                            r = e.matmul(bank(bd)[:, 0:T], wsl[sd][:, j, :], A[:, j, 0:T], start=(j == 0), stop=(j == FGS - 1))
                        return r
                    P.op("pe", mm_d, reads=[bws[sd]] + bA, writes=[bbank[bd]])
                    P.op("dve", lambda e, oc=oc, bd=bd: e.tensor_tensor(
                        out=X[:, oc, c0 + 1:c0 + 1 + T], in0=X[:, oc, c0 + 1:c0 + 1 + T], in1=bank(bd)[:, 0:T], op=ALU.add),
                        reads=[bbank[bd], bX[oc]], writes=[bX[oc]])
                    yield

        with ExitStack() as es1:
            def SB1(name, shape, dt):
                return es1.enter_context(nc.sbuf_tensor(name, shape, dt))
            ya = [SB1("ya%d" % i, [128, 4, MAXW], F32) for i in range(3)]
            bya = [Buf() for _ in range(3)]
            dT = [SB1("dT%d" % i, [128, 4, MAXT + 2], BF16) for i in range(2)]
            bdT = [Buf() for _ in range(2)]
            vld = SB1("vld", [128, MAXW], F32)
            bvld = Buf()
            cnt = [SB1("cnt%d" % i, [128, MAXW], F32) for i in range(2)]
            rcnt = SB1("rcnt", [128, 4, MAXT + 2], F32)
            bcnt2 = [Buf(), Buf()]
            brc = [Buf() for _ in range(4)]
            ptmp = SB1("ptmp", [128, MAXT + 2], F32)
            bptmp = Buf()
            qk_st = [SB1("qk%d" % i, [128, MAXT], BF16) for i in range(2)]
            bqk = [Buf() for _ in range(2)]
            vT = [SB1("vT%d" % i, [128, MAXT], BF16) for i in range(2)]
            bvT = [Buf() for _ in range(2)]
            vtok = [SB1("vtok%d" % i, [128, 4, 512], BF16) for i in range(2)]
            bvtok = [Buf() for _ in range(2)]
            Etmp = [SB1("Etmp%d" % i, [128, 12, 128], F32) for i in range(2)]
            bEtmp = [Buf() for _ in range(2)]
            cmask = SB1("cmask", [128, 128], F32)
            bcm = Buf()

            def build_etables(heads):
                if heads and heads[0] == 0:
                    P.op("sp", lambda e: e.dma_start(out=cmask[:, :], in_=cmin), writes=[bcm], dma=True)
                for h in heads:
                    t = r2["Etmp"].next()
                    for kr in range(2):
                        for qr in range(2):
                            o_first = -6 + kr - qr
                            src = bass.AP(rpbF_h, (h * 15 + o_first + 7) * 64 * 32 + 15, [[31, 64], [2 * 64 * 32, 7], [1, 64]])
                            P.op("sp", lambda e, t=t, kr=kr, qr=qr, src=src: e.dma_start(
                                out=Etmp[t][kr * 64:kr * 64 + 64, 0:7, qr * 64:qr * 64 + 64], in_=src),
                                writes=[bEtmp[t]], dma=True)
                    P.op("act", lambda e, t=t: e.activation(Etmp[t][:, 0:7, :], Etmp[t][:, 0:7, :], AF.Exp),
                         reads=[bEtmp[t]], writes=[bEtmp[t]])

                    P.op("dve", lambda e, t=t: e.tensor_tensor(out=Etmp[t][:, 0:7, :], in0=Etmp[t][:, 0:7, :],
                                                               in1=cmask[:, :].unsqueeze(1).to_broadcast([128, 7, 128]), op=ALU.mult),
                         reads=[bEtmp[t], bcm], writes=[bEtmp[t]])
                    P.op("dve", lambda e, t=t: e.tensor_copy(Etmp[t][:, 7:12, :], Etmp[t][:, 1:6, :]),
                         reads=[bEtmp[t]], writes=[bEtmp[t]])

                    def f_e(e, t=t):
                        e.memset(Etmp[t][0:64, 7, 64:128], 0.0)
                        e.memset(Etmp[t][64:128, 11, :], 0.0)
                        return e.memset(Etmp[t][0:64, 11, 0:64], 0.0)
                    P.op("dve", f_e, reads=[bEtmp[t]], writes=[bEtmp[t]])
                    P.op("sp", lambda e, t=t, h=h: e.dma_start(out=s_E[h].rearrange("p (i q) -> p i q", q=128), in_=Etmp[t][:, :, :]),
                         reads=[bEtmp[t]], dma=True)


            etab_done = [0]
            rest = []
            rest.append(conv_wo)
            for fg in range(NFG):
                fcs = list(range(fg * FGS, (fg + 1) * FGS))
                rest.append(lambda fcs=fcs: conv_up(1, fcs + [NFC + f for f in fcs]))
                rest.append(lambda fg=fg: conv_dn(1, [fg]))

            prefetched = {}
            for t_, b_ in ((cnt[0], bcnt2[0]), (cnt[1], bcnt2[1]), (ya[0], bya[0]), (ya[1], bya[1]), (ya[2], bya[2])):
                P.op("pool", lambda e, t_=t_: e.memset(t_[tuple(slice(None) for _ in t_.shape)], 0.0), writes=[b_])
            def tile_geom(ti):
                lr0, nr = P1_TILES[ti]
                T = nr * GW
                return T, T + 2 * XH, lr0 * GW, P1_OFF[ti], T + 2, XH - 1

            def gen_in(ti):
                T, W, lt0, xoff, PW, pc0 = tile_geom(ti)
                P.op("sp", lambda e, xoff=xoff, W=W: e.dma_start(out=vld[:, 0:W], in_=vldin[:, xoff:xoff + W]),
                     writes=[bvld], dma=True)
                for bi, (b0, nb) in enumerate(inblocks(W)):
                    if (ti, bi) in prefetched:
                        ts = prefetched[(ti, bi)]
                    else:
                        ts = r2["tin"].next()
                        P.op("sp", lambda e, ts=ts, b0=b0, nb=nb, xoff=xoff: e.dma_start(
                            out=tin[ts][0:nb, :], in_=xin[xoff + b0:xoff + b0 + nb, :]), writes=[btin[ts]], dma=True)
                    for q in range(4):
                        bk = bank_ring.next()

                        def f_tr(e, ts=ts, nb=nb, q=q, bk=bk):
                            r = None
                            for j in range(4):
                                dc = 4 * q + j
                                r = e.transpose(bank(bk)[:, j * 128:j * 128 + nb], tin[ts][0:nb, dc * 128:(dc + 1) * 128],
                                                ident[0:nb, 0:nb])
                            return r
                        P.op("pe", f_tr, reads=[btin[ts], bconst], writes=[bbank[bk]])
                        src = bank(bk).rearrange("p (j t) -> p j t", t=128)[:, :, 0:nb]
                        if evac_rr.next() == "act":
                            P.op("act", lambda e, q=q, b0=b0, nb=nb, src=src: e.activation(
                                X[:, 4 * q:4 * q + 4, b0:b0 + nb], src, AF.Identity),
                                reads=[bbank[bk]], writes=bX[4 * q:4 * q + 4])
                        else:
                            P.op("dve", lambda e, q=q, b0=b0, nb=nb, src=src: e.tensor_copy(
                                X[:, 4 * q:4 * q + 4, b0:b0 + nb], src),
                                reads=[bbank[bk]], writes=bX[4 * q:4 * q + 4])
                yield
                norm_stats(0, W)

                src_t, src_b = vld, bvld
                for gi in range(4):
                    dst_t, dst_b = cnt[gi % 2], bcnt2[gi % 2]
                    sh = (1 << gi) >> 1
                    if gi == 0:
                        P.op("dve", lambda e, dst_t=dst_t, src_t=src_t, W=W: e.tensor_tensor(
                            out=dst_t[:, 1:W], in0=src_t[:, 0:W - 1], in1=src_t[:, 1:W], op=ALU.add),
                            reads=[src_b], writes=[dst_b])
                    else:
                        P.op("dve", lambda e, dst_t=dst_t, src_t=src_t, W=W, sh=sh: e.tensor_tensor(
                            out=dst_t[:, sh:W - sh], in0=src_t[:, 0:W - 2 * sh], in1=src_t[:, 2 * sh:W], op=ALU.add),
                            reads=[src_b], writes=[dst_b])
                    P.op("dve", lambda e, gi=gi, dst_t=dst_t, PW=PW, pc0=pc0: e.tensor_scalar(
                        out=rcnt[:, gi, 0:PW], in0=dst_t[:, pc0:pc0 + PW], scalar1=1.0, scalar2=None, op0=ALU.max),
                        reads=[dst_b], writes=[brc[gi]])
                    P.op("dve", lambda e, gi=gi, PW=PW: e.reciprocal(rcnt[:, gi, 0:PW], rcnt[:, gi, 0:PW]),
                         reads=[brc[gi]], writes=[brc[gi]])
                    src_t, src_b = dst_t, dst_b
                yield
                for g in range(4):
                    y0, y1, y2 = ya[0], ya[1], ya[2]

                    P.op("dve", lambda e, g=g, W=W: e.tensor_tensor(
                        out=y0[:, :, 0:W], in0=X[:, 4 * g:4 * g + 4, 0:W],
                        in1=rstd[:, 0:W].unsqueeze(1).to_broadcast([128, 4, W]), op=ALU.mult),
                        reads=bX[4 * g:4 * g + 4] + [brstd], writes=[bya[0]])
                    P.op("dve", lambda e, W=W: e.tensor_tensor(out=y1[:, :, 1:W], in0=y0[:, :, 0:W - 1], in1=y0[:, :, 1:W], op=ALU.add),
                         reads=[bya[0]], writes=[bya[1]])
                    si, di = 1, 2
                    for gi in range(1, g + 1):
                        sh = 1 << (gi - 1)
                        P.op("dve", lambda e, si=si, di=di, sh=sh, W=W: e.tensor_tensor(
                            out=ya[di][:, :, sh:W - sh], in0=ya[si][:, :, 0:W - 2 * sh], in1=ya[si][:, :, 2 * sh:W], op=ALU.add),
                            reads=[bya[si]], writes=[bya[di]])
                        si, di = di, si
                    res_buf, bres = ya[di], bya[di]
                    P.op("dve", lambda e, si=si, di=di, g=g, PW=PW, pc0=pc0: e.tensor_tensor(
                        out=ya[di][:, :, 0:PW], in0=ya[si][:, :, pc0:pc0 + PW],
                        in1=rcnt[:, g, 0:PW].unsqueeze(1).to_broadcast([128, 4, PW]), op=ALU.mult),
                        reads=[bya[si], brc[g]], writes=[bya[di]])
                    P.op("dve", lambda e, di=di, PW=PW, pc0=pc0: e.tensor_tensor(
                        out=ya[di][:, :, 0:PW], in0=ya[di][:, :, 0:PW], in1=y0[:, :, pc0:pc0 + PW], op=ALU.subtract),
                        reads=[bya[di], bya[0]], writes=[bya[di]])
                    dsl = r2["dT"].next()
                    for j in range(4):
                        P.op("dve", lambda e, j=j, g=g, dsl=dsl, res_buf=res_buf, PW=PW: e.tensor_scalar(
                            out=dT[dsl][:, j, 0:PW], in0=res_buf[:, j, 0:PW], scalar1=pp[:, O_MIXN + 4 * g + j:O_MIXN + 4 * g + j + 1],
                            scalar2=None, op0=ALU.mult), reads=[bres, bpp], writes=[bdT[dsl]])
                    yield
                    for oc in range(4):
                        s = load_panel(s_wpool[g * 4 + oc], bp_pool[g * 4 + oc], nk=4)
                        blks = colblocks(PW)
                        bks = [bank_ring.next() for _ in blks]

                        def mm_p(e, s=s, dsl=dsl, blks=blks, bks=bks):
                            r = None
                            for kc in range(4):
                                for (b0, bw), bk in zip(blks, bks):
                                    r = e.matmul(bank(bk)[:, 0:bw], wsl[s][:, kc, :], dT[dsl][:, kc, b0:b0 + bw],
                                                 start=(kc == 0), stop=(kc == 3))
                            return r
                        P.op("pe", mm_p, reads=[bws[s], bdT[dsl]], writes=[bbank[k] for k in bks])
                        dc = 4 * g + oc
                        for (b0, bw), bk in zip(blks, bks):
                            P.op("act", lambda e, b0=b0, bw=bw, bk=bk, dc=dc: e.activation(
                                ptmp[:, b0:b0 + bw], bank(bk)[:, 0:bw], AF.Identity, bias=pp[:, O_PBS + dc:O_PBS + dc + 1],
                                scale=pp[:, O_PS + dc:O_PS + dc + 1]), reads=[bbank[bk], bpp], writes=[bptmp])
                        P.op("dve", lambda e, dc=dc, PW=PW, pc0=pc0: e.tensor_tensor(
                            out=X[:, dc, pc0:pc0 + PW], in0=X[:, dc, pc0:pc0 + PW], in1=ptmp[:, 0:PW], op=ALU.add),
                            reads=[bptmp, bX[dc]], writes=[bX[dc]])
                    yield

            def stage_mid(ti):
                T, W, lt0, xoff, PW, pc0 = tile_geom(ti)

                def f_ez(e, pc0=pc0, PW=PW):
                    e.tensor_scalar(out=X[:, :, pc0:pc0 + 1], in0=X[:, :, pc0:pc0 + 1], scalar1=vld[:, pc0:pc0 + 1],
                                    scalar2=None, op0=ALU.mult)
                    return e.tensor_scalar(out=X[:, :, pc0 + PW - 1:pc0 + PW], in0=X[:, :, pc0 + PW - 1:pc0 + PW],
                                           scalar1=vld[:, pc0 + PW - 1:pc0 + PW], scalar2=None, op0=ALU.mult)
                P.op("dve", f_ez, reads=bX + [bvld], writes=bX)
                ffn(pc0, T, 0)
                P.op("act", lambda e, lt0=lt0, T=T: e.dma_start(
                    out=s_x1T.rearrange("(dc p) t -> p dc t", p=128)[:, :, lt0:lt0 + T], in_=X[:, :, XH:XH + T]),
                    reads=bX, dma=True)
                if p1_sel is None and ti + 1 < len(P1_TILES):
                    nW = P1_TILES[ti + 1][1] * GW + 2 * XH
                    nxo = P1_OFF[ti + 1]
                    for bi, (b0, nb) in enumerate(inblocks(nW)[:2]):
                        ts = r2["tin"].next()
                        prefetched[(ti + 1, bi)] = ts
                        P.op("sp", lambda e, ts=ts, b0=b0, nb=nb, nxo=nxo: e.dma_start(
                            out=tin[ts][0:nb, :], in_=xin[nxo + b0:nxo + b0 + nb, :]), writes=[btin[ts]], dma=True)
                norm_stats(XH, T)
                norm_apply(XH, T, O_MIXN + 16, U, 0, bU)

            def gen_qkv(ti):
                T, W, lt0, xoff, PW, pc0 = tile_geom(ti)
                vk = None
                for oc in range(48):
                    s = load_panel(s_wqkv[oc], bp_qkv[oc])
                    bk = bank_ring.next()

                    def mm_q(e, s=s, bk=bk, T=T):
                        r = None
                        for dc in range(NDC):
                            r = e.matmul(bank(bk)[:, 0:T], wsl[s][:, dc, :], U[:, dc, 0:T], start=(dc == 0), stop=(dc == 15))
                        return r
                    P.op("pe", mm_q, reads=[bws[s]] + bU, writes=[bbank[bk]])
                    if oc < 32:
                        st = r2["qk"].next()
                        if oc < 16:
                            bias, scl, dst = pp[:, O_BQS + oc:O_BQS + oc + 1], float(DH ** -0.5), s_QT
                        else:
                            bias, scl, dst = pp[:, O_BK + oc - 16:O_BK + oc - 15], 1.0, s_KT
                        P.op("act", lambda e, st=st, bk=bk, bias=bias, scl=scl, T=T: e.activation(
                            qk_st[st][:, 0:T], bank(bk)[:, 0:T], AF.Identity, bias=bias, scale=scl),
                            reads=[bbank[bk], bpp], writes=[bqk[st]])
                        h = oc % 16
                        P.op("act", lambda e, st=st, dst=dst, h=h, lt0=lt0, T=T: e.dma_start(
                            out=dst[h * 128:(h + 1) * 128, lt0:lt0 + T], in_=qk_st[st][:, 0:T]), reads=[bqk[st]], dma=True)
                    else:
                        h = oc - 32
                        vs = r2["vT"].next()
                        P.op("act", lambda e, vs=vs, bk=bk, h=h, T=T: e.activation(
                            vT[vs][:, 0:T], bank(bk)[:, 0:T], AF.Identity, bias=pp[:, O_BV + h:O_BV + h + 1], scale=1.0),
                            reads=[bbank[bk], bpp], writes=[bvT[vs]])
                        nblk = T // 128
                        bk2 = bank_ring.next()
                        pbf = bank(bk2).bitcast(BF16)

                        def f_vtr(e, vs=vs, pbf=pbf, nblk=nblk):
                            r = None
                            for b in range(nblk):
                                r = e.transpose(pbf[:, b * 128:(b + 1) * 128], vT[vs][:, b * 128:(b + 1) * 128], identb[:, :])
                            return r
                        P.op("pe", f_vtr, reads=[bvT[vs], bconst], writes=[bbank[bk2]])
                        if h % 4 == 0:
                            vk = r2["vtok"].next()
                        hq = h % 4
                        P.op("dve", lambda e, vk=vk, hq=hq, pbf=pbf, nblk=nblk: e.tensor_copy(
                            vtok[vk][:, 0:nblk, hq * 128:(hq + 1) * 128],
                            pbf[:, 0:nblk * 128].rearrange("p (b c) -> p b c", c=128)),
                            reads=[bbank[bk2]], writes=[bvtok[vk]])
                        if hq == 3:
                            hg = h // 4
                            P.op("act", lambda e, vk=vk, hg=hg, lt0=lt0, T=T, nblk=nblk: e.dma_start(
                                out=s_V[lt0:lt0 + T, hg * 512:(hg + 1) * 512].rearrange("(b p) c -> p b c", p=128),
                                in_=vtok[vk][:, 0:nblk, :]), reads=[bvtok[vk]], dma=True)
                    yield

            tiles = [ti for ti in range(len(P1_TILES)) if p1_sel is None or ti in p1_sel]
            drain(gen_in(tiles[0]))
            stage_mid(tiles[0])
            for k, ti in enumerate(tiles):
                gq = gen_qkv(ti)
                if k + 1 < len(tiles):
                    ga = gen_in(tiles[k + 1])
                    for step in (("q", 4), ("a", 1), ("q", 4), ("a", 1), ("q", 4), ("a", 1), ("q", 6), ("a", 2),
                                 ("q", 6), ("a", 2), ("q", 7), ("a", 2), ("q", 9), ("a", 1)):
                        drain(gq if step[0] == "q" else ga, step[1])
                    drain(gq)
                    drain(ga)
                    stage_mid(tiles[k + 1])
                else:
                    drain(gq)
                if rest:
                    rest.pop(0)()
                if k >= 1 and etab_done[0] < NH:
                    build_etables(list(range(etab_done[0], etab_done[0] + 2)))
                    etab_done[0] += 2
            while rest:
                rest.pop(0)()
            if etab_done[0] < NH:
                build_etables(list(range(etab_done[0], NH)))
            P.barrier()
        with ExitStack() as es2:
            def SB2(name, shape, dt):
                return es2.enter_context(nc.sbuf_tensor(name, shape, dt))
            Qb = SB2("Qb", [128, NH, MAXT + 2], BF16)
            bQb = Buf()
            Kb = [SB2("Kb%d" % i, [128, 2, 1152], BF16) for i in range(2)]
            bKb = [Buf() for _ in range(2)]
            Vb = [SB2("Vb%d" % i, [128, 9, 256], BF16) for i in range(2)]
            bVb = [Buf() for _ in range(2)]
            Eb = [SB2("Eb%d" % i, [128, 12, 128], F32) for i in range(2)]
            bEb = [Buf() for _ in range(2)]
            pexp = [SB2("pexp%d" % i, [128, 6, 128], F32) for i in range(2)]
            bpexp = [Buf() for _ in range(2)]
            PT = [SB2("PT%d" % i, [128, 6, 128], BF16) for i in range(2)]
            bPT = [Buf() for _ in range(2)]
            rden = [SB2("rden%d" % i, [128, 128], F32) for i in range(2)]
            brden = [Buf() for _ in range(2)]
            mk = SB2("mk", [128, 12, 6, 2], F32)
            ev = SB2("ev", [128, 2 * N_P2], F32)
            bmk = Buf()
            P.op("sp", lambda e: e.dma_start(out=mk[:, :, :, :], in_=mkin.rearrange("p (a b c) -> p a b c", b=6, c=2)),
                 writes=[bmk], dma=True)
            P.op("sp", lambda e: e.dma_start(out=ev[:, :], in_=evin), writes=[bmk], dma=True)
            OT = SB2("OT", [128, NH, MAXT + 2], BF16)
            bOT = [Buf() for _ in range(NH)]
            bod = [Buf(), Buf()]
            bank_ring.items = [4, 5, 6]
            out_stores = []

            pendB = [None]

            def attn(h, hh, ks, vs, es_, chunks, e0, qa, nq, q_ap, o_ap, mk_ap):
                n = len(chunks)
                sp_ = 0
                S = psT[sp_]

                def mm_s(e):
                    r = None
                    for t, j in enumerate(chunks):
                        r = e.matmul(S[:, t * 128:t * 128 + nq], Kb[ks][:, hh, j * 128:(j + 1) * 128], q_ap, start=True, stop=True)
                    return r
                P.op("pe", mm_s, reads=[bKb[ks], bQb], writes=[bbank[2 * sp_], bbank[2 * sp_ + 1]])
                px, pt, rd = r2["pexp"].next(), r2["PT"].next(), r2["rden"].next()

                def f_exp(e):
                    return e.activation(pexp[px][:, 0:n, 0:nq], S[:, 0:n * 128].rearrange("p (t q) -> p t q", q=128)[:, :, 0:nq], AF.Exp)
                P.op("act", f_exp, reads=[bbank[2 * sp_], bbank[2 * sp_ + 1]], writes=[bpexp[px]])

                P.op("dve", lambda e: e.tensor_tensor(out=PT[pt][:, 0:n, 0:nq], in0=pexp[px][:, 0:n, 0:nq],
                                                      in1=Eb[es_][:, e0:e0 + n, qa:qa + nq], op=ALU.mult),
                     reads=[bpexp[px], bEb[es_]], writes=[bPT[pt]])
                if mk_ap is not None:
                    P.op("dve", lambda e: e.tensor_tensor(out=PT[pt][:, 0:n, :].rearrange("p t (r c) -> p t r c", r=2),
                                                          in0=PT[pt][:, 0:n, :].rearrange("p t (r c) -> p t r c", r=2),
                                                          in1=mk_ap.unsqueeze(3).to_broadcast([128, n, 2, 64]), op=ALU.mult),
                         reads=[bPT[pt], bmk], writes=[bPT[pt]])

                def partB():
                    ob = 2 + r2["od"].next()
                    ocol = 0
                    bo_ = bbank[ob]

                    def mm_o(e):
                        for t, j in enumerate(chunks):
                            e.matmul(bank(ob)[:, ocol:ocol + nq], Vb[vs][:, j, hh * 128:(hh + 1) * 128], PT[pt][:, t, 0:nq],
                                     start=(t == 0), stop=(t == n - 1))
                        r = None
                        for t in range(n):
                            r = e.matmul(bank(ob)[:, ocol + 128:ocol + 128 + nq], ones[:, :], PT[pt][:, t, 0:nq], start=(t == 0), stop=(t == n - 1))
                        return r
                    P.op("pe", mm_o, reads=[bVb[vs], bPT[pt], bconst], writes=[bo_])
                    P.op("act", lambda e: e.activation(rden[rd][:, 0:nq], bank(ob)[:, ocol + 128:ocol + 128 + nq], AF.Ln),
                         reads=[bo_], writes=[brden[rd]])
                    P.op("act", lambda e: e.activation(rden[rd][:, 0:nq], rden[rd][:, 0:nq], AF.Exp, scale=-1.0),
                         reads=[brden[rd]], writes=[brden[rd]])
                    P.op("dve", lambda e: e.tensor_tensor(out=o_ap, in0=bank(ob)[:, ocol:ocol + nq], in1=rden[rd][:, 0:nq], op=ALU.mult),
                         reads=[bo_, brden[rd]], writes=[bOT[h]])
                if pendB[0] is not None:
                    pendB[0]()
                pendB[0] = partB

            def attn_flush():
                if pendB[0] is not None:
                    pendB[0]()
                    pendB[0] = None

            T = MAXT

            def gen_attn(i):
                q0 = HALO_TOP + 8 * i
                tq0 = q0 * GW
                wt0 = (q0 - 6) * GW
                P.op("sp", lambda e, tq0=tq0: e.dma_start(
                    out=Qb[:, :, 0:T + 2], in_=s_QT.rearrange("(h p) t -> p h t", p=128)[:, :, tq0 - 1:tq0 + T + 1]),
                    writes=[bQb], dma=True)
                gmod = (8 * i) % 32
                for hg in range(8):
                    ks, vs = r2["Kb"].next(), r2["Vb"].next()
                    P.op("sp", lambda e, ks=ks, hg=hg, wt0=wt0: e.dma_start(
                        out=Kb[ks][:, :, :], in_=s_KT[hg * 256:(hg + 1) * 256, wt0:wt0 + 1152].rearrange("(h p) t -> p h t", p=128)),
                        writes=[bKb[ks]], dma=True)
                    P.op("sp", lambda e, vs=vs, hg=hg, wt0=wt0: e.dma_start(
                        out=Vb[vs][:, :, :], in_=s_V[wt0:wt0 + 1152, hg * 256:(hg + 1) * 256].rearrange("(j p) c -> p j c", p=128)),
                        writes=[bVb[vs]], dma=True)
                    for hh in range(2):
                        h = 2 * hg + hh
                        es_ = r2["Eb"].next()
                        P.op("sp", lambda e, es_=es_, h=h: e.dma_start(
                            out=Eb[es_][:, :, :], in_=s_E[h].rearrange("p (i q) -> p i q", q=128)), writes=[bEb[es_]], dma=True)
                        attn(h, hh, ks, vs, es_, [0, 1, 2, 3, 4], 7, 127, 1, Qb[:, h, 0:1], OT[:, h, 0:1], None)
                        yield
                        for k in range(4):
                            mk_ap = None
                            if gmod == 0 and k < 2:
                                chunks, e0 = list(range(k + 1, k + 7)), 1
                                mk_ap = mk[:, (i // 4) * 4 + k, :, :]
                            elif gmod == 24 and k >= 2:
                                chunks, e0 = list(range(k, k + 6)), 0
                                mk_ap = mk[:, (i // 4) * 4 + k, :, :]
                            else:
                                chunks, e0 = list(range(k + 1, k + 6)), 7
                            attn(h, hh, ks, vs, es_, chunks, e0, 0, 128, Qb[:, h, 1 + 128 * k:1 + 128 * k + 128],
                                 OT[:, h, 1 + 128 * k:1 + 128 * k + 128], mk_ap)
                            yield
                        attn(h, hh, ks, vs, es_, [5, 6, 7, 8], 7, 0, 1, Qb[:, h, T + 1:T + 2], OT[:, h, T + 1:T + 2], None)
                        yield
                attn_flush()

            def wo_stage(i):
                tq0 = (HALO_TOP + 8 * i) * GW
                P.op("sp", lambda e, tq0=tq0: e.dma_start(
                    out=X[:, :, 0:T + 2], in_=s_x1T.rearrange("(dc p) t -> p dc t", p=128)[:, :, tq0 - 1:tq0 + T + 1]),
                    writes=bX, dma=True)
                for oc in range(NDC):
                    s = load_panel(s_wo[oc], bp_wo[oc])
                    bk, ek = bank_ring.next(), edge_ring.next()

                    eap = edge_ap(ek)

                    def mm_wo(e, s=s, bk=bk, eap=eap):
                        for hc in range(NH):
                            e.matmul(bank(bk)[:, 0:T], wsl[s][:, hc, :], OT[:, hc, 1:1 + T], start=(hc == 0), stop=(hc == 15))
                        r = None
                        for hc in range(NH):
                            r = e.matmul(eap, wsl[s][:, hc, :], OT[:, hc, 0:T + 2:T + 1], start=(hc == 0), stop=(hc == 15))
                        return r
                    P.op("pe", mm_wo, reads=[bws[s]] + bOT, writes=[bbank[bk], bedge[ek]])

                    def f_wo(e, oc=oc, bk=bk, eap=eap):
                        e.tensor_tensor(out=X[:, oc, 1:1 + T], in0=X[:, oc, 1:1 + T], in1=bank(bk)[:, 0:T], op=ALU.add)
                        return e.tensor_tensor(out=X[:, oc, 0:T + 2:T + 1], in0=X[:, oc, 0:T + 2:T + 1], in1=eap, op=ALU.add)
                    P.op("dve", f_wo, reads=[bbank[bk], bedge[ek], bX[oc]], writes=[bX[oc]])

                def f_ez2(e, i=i):
                    e.tensor_scalar(out=X[:, :, 0:1], in0=X[:, :, 0:1], scalar1=ev[:, 2 * i:2 * i + 1], scalar2=None, op0=ALU.mult)
                    return e.tensor_scalar(out=X[:, :, T + 1:T + 2], in0=X[:, :, T + 1:T + 2], scalar1=ev[:, 2 * i + 1:2 * i + 2],
                                           scalar2=None, op0=ALU.mult)
                P.op("dve", f_ez2, reads=bX + [bmk], writes=bX)
                if debug:
                    out_stores.append(P.op("pool", lambda e, i=i: e.dma_start(out=dbg_x[i], in_=X[:, :, 0:T + 2]), reads=bX, dma=True))
                    out_stores.append(P.op("pool", lambda e, i=i: e.dma_start(out=dbg_o[i], in_=OT[:, :, 0:T + 2]), reads=bOT, dma=True))

            def tail_stage(i):
                norm_stats(1, T)
                norm_apply(1, T, O_FIN, X, 1, bX)
                for tb in range(T // 128):
                    ts = r2["tin"].next()
                    for q in range(4):
                        bk = bank_ring.next()

                        def f_tro(e, tb=tb, q=q, bk=bk):
                            r = None
                            for j in range(4):
                                dc = 4 * q + j
                                r = e.transpose(bank(bk)[:, j * 128:(j + 1) * 128], X[:, dc, 1 + tb * 128:1 + (tb + 1) * 128], ident[:, :])
                            return r
                        P.op("pe", f_tro, reads=bX[4 * q:4 * q + 4] + [bconst], writes=[bbank[bk]])
                        if evac_rr.next() == "act":
                            P.op("act", lambda e, ts=ts, q=q, bk=bk: e.activation(tin[ts][:, q * 512:(q + 1) * 512], bank(bk), AF.Identity),
                                 reads=[bbank[bk]], writes=[btin[ts]])
                        else:
                            P.op("dve", lambda e, ts=ts, q=q, bk=bk: e.tensor_copy(tin[ts][:, q * 512:(q + 1) * 512], bank(bk)),
                                 reads=[bbank[bk]], writes=[btin[ts]])
                    r0 = i * 512 + tb * 128
                    out_stores.append(P.op("pool", lambda e, ts=ts, r0=r0: e.dma_start(out=yout[r0:r0 + 128, :], in_=tin[ts][:, :]),
                                           reads=[btin[ts]], dma=True))
            tiles2 = [i for i in range(N_P2) if p2_sel is None or i in p2_sel]
            drain(gen_attn(tiles2[0]))
            for k, i in enumerate(tiles2):
                wo_stage(i)
                gf = ffn_gen(0, T, 1)
                if k + 1 < len(tiles2):
                    ga = gen_attn(tiles2[k + 1])
                    fa, aa = True, True
                    while fa or aa:
                        if fa:
                            fa = drain(gf, 1)
                        if aa:
                            aa = drain(ga, 1)
                else:
                    drain(gf)
                tail_stage(i)
            P.emit(final_wait_ops=out_stores)
    return nc


def _seq_of_row(g):
    for (a, b) in SEQS:
        if a <= g < b:
            return (a, b)
    return None


def _prep_core(c, Xall, pp, consts, weights):
    base_row = ROWS_CORE * c - HALO_TOP
    xin = np.zeros((N_EXT, D), np.float32)
    vld = np.zeros((N_EXT,), np.float32)
    for ti, (lr0, nr) in enumerate(P1_TILES):
        g0 = base_row + lr0
        sq = _seq_of_row(g0)
        if sq is None:
            continue
        t_lo, t_hi = sq[0] * GW, sq[1] * GW
        e0 = g0 * GW - XH
        n = nr * GW + 2 * XH
        a = max(e0, t_lo)
        b = min(e0 + n, t_hi)
        if b > a:
            xin[P1_OFF[ti] + (a - e0):P1_OFF[ti] + (b - e0)] = Xall[a:b]
            vld[P1_OFF[ti] + (a - e0):P1_OFF[ti] + (b - e0)] = 1.0
    ev = np.zeros((2 * N_P2,), np.float32)
    mk = np.zeros((128, 12, 6, 2), np.float32)
    for i in range(N_P2):
        gq0 = ROWS_CORE * c + 8 * i
        sq = _seq_of_row(gq0)
        ev[2 * i] = 1.0 if gq0 - 1 >= sq[0] else 0.0
        ev[2 * i + 1] = 1.0 if gq0 + 8 < sq[1] else 0.0
        gmod = (8 * i) % 32
        for k in range(4):
            if gmod == 0 and k < 2:
                ws = gq0 + 2 * k - 4
            elif gmod == 24 and k >= 2:
                ws = gq0 + 2 * k - 6
            else:
                continue
            pe_i = (i // 4) * 4 + k
            for qr in range(2):
                qg = gq0 + 2 * k + qr
                s0, s1 = _seq_of_row(qg)
                rs = min(max(qg - 4, s0), s1 - 8)
                for n_ in range(12):
                    kg = ws + n_
                    m = 1.0 if rs <= kg < rs + 8 else 0.0
                    j, kr = n_ // 2, n_ % 2
                    mk[kr * 64:(kr + 1) * 64, pe_i, j, qr] = m
    d = {
        "xin": xin,
        "vldin": np.ascontiguousarray(np.broadcast_to(vld[None, :], (128, N_EXT))),
        "evin": np.ascontiguousarray(np.broadcast_to(ev[None, :], (128, 2 * N_P2))),
        "mkin": np.ascontiguousarray(mk.reshape(128, 144)),
        "ppin": pp,
    }
    d.update(consts)
    d.update(weights)
    return d


def _fm(v):
    v = np.asarray(v, np.float32)
    return np.ascontiguousarray(v.reshape(-1, 128).T)


_NC_CACHE = {}


def kernel(x_prompt, x_sample, mix_norm, pool_w, pool_b, pool_scale, attn_w_qkv, attn_b_qkv, attn_rpb, attn_w_o,
           ffn_norm, ffn_w_up, ffn_conv_w, ffn_conv_b, ffn_w_down, final_norm, _debug=False, _p1=None, _p2=None):
    f32 = lambda a: np.ascontiguousarray(np.asarray(a, dtype=np.float32))
    x_prompt, x_sample = f32(x_prompt), f32(x_sample)
    Xall = np.concatenate([x_prompt.reshape(-1, D), x_sample.reshape(-1, D)], axis=0)
    pp = np.zeros((128, NPP), np.float32)
    mix_norm, ffn_norm, final_norm = f32(mix_norm), f32(ffn_norm), f32(final_norm)
    pp[:, O_MIXN:O_MIXN + 16] = _fm(mix_norm[0])
    pp[:, O_MIXN + 16:O_MIXN + 32] = _fm(mix_norm[1])
    pp[:, O_FFNN:O_FFNN + 16] = _fm(ffn_norm[0])
    pp[:, O_FFNN + 16:O_FFNN + 32] = _fm(ffn_norm[1])
    pp[:, O_FIN:O_FIN + 16] = _fm(final_norm)
    pp[:, O_PB:O_PB + 16] = _fm(f32(pool_b).reshape(-1))
    pp[:, O_PS:O_PS + 16] = _fm(f32(pool_scale).reshape(-1))
    cw, cb = f32(ffn_conv_w), f32(ffn_conv_b)
    for l in range(2):
        for k in range(3):
            pp[:, O_CW + l * 132 + k * 44:O_CW + l * 132 + (k + 1) * 44] = _fm(cw[l, k])
        pp[:, O_CB + l * 44:O_CB + (l + 1) * 44] = _fm(cb[l])
    bqkv = f32(attn_b_qkv).reshape(-1)
    pp[:, O_BQ:O_BQ + 16] = _fm(bqkv[0:D])
    pp[:, O_BK:O_BK + 16] = _fm(bqkv[D:2 * D])
    pp[:, O_BV:O_BV + 16] = _fm(bqkv[2 * D:3 * D])
    rpb = f32(attn_rpb).reshape(NH, 15, 31)
    rf = np.zeros((NH, 15, 64, 32), np.float32)
    rf[:, :, :, 0:31] = rpb[:, :, None, ::-1]
    cq = np.arange(GW)
    cs = np.clip(cq - 8, 0, GW - 16)
    cm = ((cq[:, None] >= cs[None, :]) & (cq[:, None] < cs[None, :] + 16)).astype(np.float32)
    consts = {"rpbF": rf.reshape(-1), "cmin": np.ascontiguousarray(np.tile(cm, (2, 2))),
              "idin": np.eye(128, dtype=np.float32)}
    weights = {"w_pool": f32(pool_w).reshape(4, 512, 512), "w_qkv": f32(attn_w_qkv).reshape(D, 3 * D),
               "w_o": f32(attn_w_o).reshape(D, D), "w_up": f32(ffn_w_up), "w_dn": f32(ffn_w_down)}
    in_maps = [_prep_core(c, Xall, pp, consts, weights) for c in range(N_CORES)]
    key = (bool(_debug), str(_p1), str(_p2))
    if key not in _NC_CACHE:
        _NC_CACHE[key] = build_program(debug=_debug, p1_sel=_p1, p2_sel=_p2)
    nc = _NC_CACHE[key]
    res = run_bass_kernel_spmd(nc, in_maps, core_ids=list(range(N_CORES)))
    if _debug:
        return res
    Y = np.concatenate([np.asarray(r["yout"], dtype=np.float32) for r in res.results], axis=0)
    y_prompt = Y[0:16384].reshape(1, 16384, D)
    y_sample = Y[16384:].reshape(4, 8192, D)
    return (np.ascontiguousarray(y_prompt), np.ascontiguousarray(y_sample))
```

```python
import numpy as np
import ml_dtypes
from contextlib import ExitStack
import concourse.bass as bass
import concourse.mybir as mybir
from concourse.bass_utils import run_bass_kernel_spmd

F32 = mybir.dt.float32
BF16 = mybir.dt.bfloat16
AF = mybir.ActivationFunctionType
ALU = mybir.AluOpType

D = 2048
NDC = 16
DFF = 5632
NFC = 44
NFG = 4
FGS = 11
NH = 16
DH = 128
GW = 64
EPS = 1e-6
N_CORES = 8
ROWS_CORE = 96
HALO_TOP = 6
HALO_BOT = 4
NLR = HALO_TOP + ROWS_CORE + HALO_BOT
NLT = NLR * GW
XH = 10
SEQS = [(0, 256), (256, 384), (384, 512), (512, 640), (640, 768)]
TOT_ROWS = 768
POOL_W = (2, 4, 8, 16)

P1_TILES = [(0, HALO_TOP)] + [(HALO_TOP + 8 * k, 8) for k in range(12)] + [(HALO_TOP + 96, HALO_BOT)]
P1_EXT = [r * GW + 2 * XH for (_, r) in P1_TILES]
P1_OFF = [int(x) for x in np.cumsum([0] + P1_EXT[:-1])]
N_EXT = int(sum(P1_EXT))
N_P2 = 12
MAXT = 512
MAXW = MAXT + 2 * XH

O_MIXN, O_FFNN, O_FIN, O_PB, O_PS, O_CW, O_CB, O_BQ, O_BK, O_BV, O_PBS, O_BQS, NPP = \
    0, 32, 64, 80, 96, 112, 376, 464, 480, 496, 512, 528, 544

ENGS = ("pe", "act", "dve", "pool", "sp")
N_DMA_SEMS = 34
DMA_SHARE = {"sp": list(range(0, 16)), "pool": list(range(16, 26)), "act": list(range(26, 34)), "dve": [], "pe": []}


class Buf:
    __slots__ = ("name", "w", "r")

    def __init__(self, name=""):
        self.name = name
        self.w = None
        self.r = []


class Op:
    __slots__ = ("eng", "fn", "deps", "sig", "tok", "dma", "barrier")

    def __init__(self, eng, fn, dma):
        self.eng = eng
        self.fn = fn
        self.deps = []
        self.sig = False
        self.tok = None
        self.dma = dma
        self.barrier = None


class Prog:
    def __init__(self, nc):
        self.nc = nc
        self.ops = {e: [] for e in ENGS}
        self.nbar = 0

    def op(self, eng, fn, reads=(), writes=(), dma=False):
        o = Op(eng, fn, dma)
        deps = {}
        for b in reads:
            if b.w is not None:
                deps[id(b.w)] = b.w
        for b in writes:
            if b.w is not None:
                deps[id(b.w)] = b.w
            for rr in b.r:
                deps[id(rr)] = rr
        for d in deps.values():
            if d.eng == "pe" and eng == "pe" and not d.dma and not dma:
                continue
            d.sig = True
            o.deps.append(d)
        for b in reads:
            b.r.append(o)
        for b in writes:
            b.w = o
            b.r = []
        self.ops[eng].append(o)
        return o

    def barrier(self):
        k = self.nbar
        self.nbar += 1
        for e in ENGS:
            for o in reversed(self.ops[e]):
                if o.barrier is None and not o.dma:
                    o.sig = True
                    break
            m = Op(e, None, False)
            m.barrier = k
            self.ops[e].append(m)

    def emit(self, final_wait_ops=()):
        nc = self.nc
        with ExitStack() as es:
            esem = {e: es.enter_context(nc.semaphore("s_" + e)) for e in ENGS}
            dsems = [es.enter_context(nc.semaphore("d%d" % i)) for i in range(N_DMA_SEMS)]
            ecount = {e: 0 for e in ENGS}
            dcount = [0] * N_DMA_SEMS
            dlast = [None] * N_DMA_SEMS
            drr = {e: 0 for e in ENGS}
            prev_on_sem = {}
            snap_e = {}
            snap_d = {}
            for e in ENGS:
                for o in self.ops[e]:
                    if o.barrier is not None:
                        snap_e.setdefault(o.barrier, {})[e] = ecount[e]
                        sd = snap_d.setdefault(o.barrier, [0] * N_DMA_SEMS)
                        for k in DMA_SHARE[e]:
                            sd[k] = dcount[k]
                    elif o.dma:
                        lst = DMA_SHARE[e]
                        k = lst[drr[e] % len(lst)]
                        drr[e] += 1
                        if dlast[k] is not None:
                            prev_on_sem[id(o)] = dlast[k]
                        dcount[k] += 16
                        o.tok = (k, dcount[k])
                        dlast[k] = o
                    elif o.sig:
                        ecount[e] += 1
                        o.tok = (e, ecount[e])

            def semof(key):
                return esem[key] if isinstance(key, str) else dsems[key]

            block = es.enter_context(nc.Block())
            prog = self

            def run(e, eng):
                known = {}
                for o in prog.ops[e]:
                    if o.barrier is not None:
                        for f, v in snap_e[o.barrier].items():
                            if v > known.get(f, 0):
                                eng.wait_ge(esem[f], v)
                                known[f] = v
                        for k, v in enumerate(snap_d[o.barrier]):
                            if v > known.get(k, 0):
                                eng.wait_ge(dsems[k], v)
                                known[k] = v
                        continue
                    waits = {}
                    dl = o.deps
                    if o.dma and id(o) in prev_on_sem:
                        dl = dl + [prev_on_sem[id(o)]]
                    for d in dl:
                        key, v = d.tok
                        if known.get(key, 0) >= v:
                            continue
                        if waits.get(key, 0) < v:
                            waits[key] = v
                    for key, v in waits.items():
                        eng.wait_ge(semof(key), v)
                        known[key] = v
                    ins = o.fn(eng)
                    if o.tok is not None:
                        key, v = o.tok
                        ins.then_inc(semof(key), 16 if o.dma else 1)
                if e == "sp":
                    for o in final_wait_ops:
                        key, v = o.tok
                        eng.wait_ge(semof(key), v)

            @block.tensor
            def _(eng):
                run("pe", eng)

            @block.scalar
            def _(eng):
                run("act", eng)

            @block.vector
            def _(eng):
                run("dve", eng)

            @block.gpsimd
            def _(eng):
                run("pool", eng)

            @block.sync
            def _(eng):
                run("sp", eng)


def drain(g, n=None):
    k = 0
    while n is None or k < n:
        try:
            next(g)
        except StopIteration:
            return False
        k += 1
    return True


def inblocks(n):
    out, s_ = [], 0
    while s_ < n:
        w = min(128, n - s_)
        out.append((s_, w))
        s_ += w
    return out


def colblocks(n, maxw=512):
    k = -(-n // maxw)
    base, rem = n // k, n % k
    out, s = [], 0
    for i in range(k):
        w = base + (1 if i < rem else 0)
        out.append((s, w))
        s += w
    return out


class Ring:
    def __init__(self, items):
        self.items = items
        self.i = 0

    def next(self):
        it = self.items[self.i % len(self.items)]
        self.i += 1
        return it


def build_program(debug=False, p1_sel=None, p2_sel=None):
    nc = bass.Bass("TRN2", target_bir_lowering=False)
    dt_in = lambda name, shape, dt=F32: nc.dram_tensor(name, shape, dt, kind="ExternalInput")
    skind = dict(kind="ExternalOutput") if debug else {}
    xin = dt_in("xin", [N_EXT, D]).ap()
    vldin = dt_in("vldin", [128, N_EXT]).ap()
    evin = dt_in("evin", [128, 2 * N_P2]).ap()
    mkin = dt_in("mkin", [128, 12 * 12]).ap()
    ppin = dt_in("ppin", [128, NPP]).ap()
    rpbF_h = dt_in("rpbF", [NH * 15 * 64 * 32])
    cmin = dt_in("cmin", [128, 128]).ap()
    idin = dt_in("idin", [128, 128]).ap()
    w_pool = dt_in("w_pool", [4, 512, 512]).ap()
    w_qkv = dt_in("w_qkv", [D, 3 * D]).ap()
    w_o = dt_in("w_o", [D, D]).ap()
    w_up = dt_in("w_up", [2, D, 2 * DFF]).ap()
    w_dn = dt_in("w_dn", [2, DFF, D]).ap()
    yout = nc.dram_tensor("yout", [ROWS_CORE * GW, D], F32, kind="ExternalOutput").ap()
    s_wpool = nc.dram_tensor("s_wpool", [16, 128, 4, 128], BF16).ap()
    s_wup = nc.dram_tensor("s_wup", [2, 88, 128, 16, 128], BF16).ap()
    s_wdn = nc.dram_tensor("s_wdn", [2, 16, NFG, 128, FGS, 128], BF16).ap()
    s_wqkv = nc.dram_tensor("s_wqkv", [48, 128, 16, 128], BF16).ap()
    s_wo = nc.dram_tensor("s_wo", [16, 128, 16, 128], BF16).ap()
    s_x1T = nc.dram_tensor("s_x1T", [D, NLT], F32, **skind).ap()
    s_QT = nc.dram_tensor("s_QT", [D, NLT], BF16, **skind).ap()
    s_KT = nc.dram_tensor("s_KT", [D, NLT], BF16, **skind).ap()
    s_V = nc.dram_tensor("s_V", [NLT, D], BF16, **skind).ap()
    s_E = nc.dram_tensor("s_E", [NH, 128, 12 * 128], F32, **skind).ap()
    if debug:
        dbg_x = nc.dram_tensor("dbg_x", [N_P2, 128, NDC, MAXT + 2], F32, kind="ExternalOutput").ap()
        dbg_o = nc.dram_tensor("dbg_o", [N_P2, 128, NDC, MAXT + 2], BF16, kind="ExternalOutput").ap()

    with ExitStack() as es:
        def SB(name, shape, dt):
            return es.enter_context(nc.sbuf_tensor(name, shape, dt))

        P = Prog(nc)
        X = SB("X", [128, NDC, MAXW], F32)
        U = SB("U", [128, NDC, MAXW], BF16)
        A = SB("A", [128, FGS, MAXT], BF16)
        NWS = 7
        wsl = [SB("wsl%d" % i, [128, 16, 128], BF16) for i in range(NWS)]
        tin = [SB("tin%d" % i, [128, D], F32) for i in range(2)]
        gsb = [SB("gsb%d" % i, [128, MAXT + 2], F32) for i in range(2)]
        cb_ = [SB("cb%d" % i, [128, MAXT], F32) for i in range(2)]
        ge_ = [SB("ge%d" % i, [128, MAXT], F32) for i in range(2)]
        sq_ = [SB("sq%d" % i, [128, 4, MAXW], BF16) for i in range(2)]
        rstd = SB("rstd", [128, MAXW], F32)
        Ue = SB("Ue", [128, NDC, 2], BF16)
        pp = SB("pp", [128, NPP], F32)
        ident = SB("ident", [128, 128], F32)
        identb = SB("identb", [128, 128], BF16)
        ones = SB("ones", [128, 128], BF16)
        epst = SB("epst", [128, 1], F32)
        psT = [es.enter_context(nc.psum_tensor("ps%d" % i, [128, 1024], F32)) for i in range(4)]

        def bank(b):
            return psT[b // 2][:, (b % 2) * 512:(b % 2) * 512 + 512]

        bX = [Buf("X%d" % i) for i in range(NDC)]
        bU = [Buf("U%d" % i) for i in range(NDC)]
        bA = [Buf("A%d" % i) for i in range(FGS)]
        bws = [Buf("ws%d" % i) for i in range(NWS)]
        btin = [Buf() for _ in range(2)]
        bgsb = [Buf() for _ in range(2)]
        bcb = [Buf() for _ in range(2)]
        bge = [Buf() for _ in range(2)]
        bsq = [Buf() for _ in range(2)]
        brstd = Buf("rstd")
        bUe = Buf("Ue")
        bpp = Buf("pp")
        bconst = Buf("const")
        bbank = [Buf("bank%d" % i) for i in range(8)]
        bedge = [Buf("edge%d" % i) for i in range(8)]
        ws_ring = Ring(list(range(NWS)))
        bank_ring = Ring(list(range(7)))
        edge_ring = Ring(list(range(8)))
        r2 = {n: Ring([0, 1]) for n in ("tin", "gsb", "cb", "ge", "sq", "dT", "vT", "vtok", "Etmp", "qk",
                                         "Kb", "Vb", "Eb", "pexp", "PT", "rden", "S", "od")}
        evac_rr = Ring(["act", "dve"])

        edge_loc = [3, 512]

        def edge_ap(k):
            return psT[edge_loc[0]][:, edge_loc[1] + 2 * k: edge_loc[1] + 2 * k + 2]

        P.op("sp", lambda e: e.dma_start(out=pp[:, :], in_=ppin), writes=[bpp], dma=True)
        P.op("sp", lambda e: e.dma_start(out=ident[:, :], in_=idin), writes=[bconst], dma=True)

        P.op("dve", lambda e: e.tensor_copy(identb[:, :], ident[:, :]), reads=[bconst], writes=[bconst])
        P.op("dve", lambda e: e.memset(ones[:, :], 1.0), writes=[bconst])
        P.op("dve", lambda e: e.memset(epst[:, :], EPS), writes=[bconst])
        P.op("dve", lambda e: e.tensor_tensor(out=pp[:, O_PBS:O_PBS + 16], in0=pp[:, O_PB:O_PB + 16],
                                              in1=pp[:, O_PS:O_PS + 16], op=ALU.mult), reads=[bpp], writes=[bpp])
        P.op("dve", lambda e: e.tensor_scalar(out=pp[:, O_BQS:O_BQS + 16], in0=pp[:, O_BQ:O_BQ + 16],
                                              scalar1=float(DH ** -0.5), scalar2=None, op0=ALU.mult), reads=[bpp], writes=[bpp])

        bp_pool = [Buf() for _ in range(16)]
        bp_up = [[Buf() for _ in range(88)] for _ in range(2)]
        bp_dn = [[[Buf() for _ in range(NFG)] for _ in range(16)] for _ in range(2)]
        bp_qkv = [Buf() for _ in range(48)]
        bp_wo = [Buf() for _ in range(16)]

        def cast(dst, src, b):
            P.op("pool", lambda e: e.dma_start(out=dst, in_=src), writes=[b], dma=True)

        def conv_pool_w():
            for g in range(4):
                v = w_pool[g].rearrange("(kc p) (oc m) -> oc p kc m", p=128, m=128)
                for oc in range(4):
                    cast(s_wpool[g * 4 + oc], v[oc], bp_pool[g * 4 + oc])

        def conv_up(l, fcs):
            v = w_up[l].rearrange("(dc p) (fc m) -> fc p dc m", p=128, m=128)
            for fc in fcs:
                cast(s_wup[l, fc], v[fc], bp_up[l][fc])

        def conv_dn(l, fgs):
            v = w_dn[l].rearrange("(fg j p) (oc m) -> oc fg p j m", j=FGS, p=128, m=128)
            for fg in fgs:
                for oc in range(16):
                    cast(s_wdn[l, oc, fg], v[oc, fg], bp_dn[l][oc][fg])

        def conv_qkv():
            v = w_qkv.rearrange("(dc p) (oc m) -> oc p dc m", p=128, m=128)
            for oc in range(48):
                cast(s_wqkv[oc], v[oc], bp_qkv[oc])

        def conv_wo():
            v = w_o.rearrange("(dc p) (oc m) -> oc p dc m", p=128, m=128)
            for oc in range(16):
                cast(s_wo[oc], v[oc], bp_wo[oc])

        conv_pool_w()
        for fg in range(NFG):
            fcs = list(range(fg * FGS, (fg + 1) * FGS))
            conv_up(0, fcs + [NFC + f for f in fcs])
            conv_dn(0, [fg])
        conv_qkv()

        def load_panel(src, b_src, nk=16):
            s = ws_ring.next()
            P.op("sp", lambda e: e.dma_start(out=wsl[s][:, 0:nk, :], in_=src), reads=[b_src], writes=[bws[s]],
                 dma=True)
            return s

        def norm_stats(c0, W):
            blks = colblocks(W)
            bks = [bank_ring.next() for _ in blks]
            for q in range(4):
                s = r2["sq"].next()
                if q % 2 == 0:
                    P.op("dve", lambda e, s=s, q=q: e.tensor_tensor(out=sq_[s][:, :, 0:W], in0=X[:, 4 * q:4 * q + 4, c0:c0 + W],
                                                                    in1=X[:, 4 * q:4 * q + 4, c0:c0 + W], op=ALU.mult),
                         reads=bX[4 * q:4 * q + 4], writes=[bsq[s]])
                else:
                    P.op("act", lambda e, s=s, q=q: e.activation(sq_[s][:, :, 0:W], X[:, 4 * q:4 * q + 4, c0:c0 + W], AF.Square),
                         reads=bX[4 * q:4 * q + 4], writes=[bsq[s]])

                def mm(e, s=s, q=q):
                    r = None
                    for j in range(4):
                        for (b0, bw), bk in zip(blks, bks):
                            r = e.matmul(bank(bk)[:, 0:bw], ones[:, :], sq_[s][:, j, b0:b0 + bw],
                                         start=(q == 0 and j == 0), stop=(q == 3 and j == 3))
                    return r
                P.op("pe", mm, reads=[bsq[s], bconst], writes=[bbank[k] for k in bks])
            for (b0, bw), bk in zip(blks, bks):
                P.op("act", lambda e, b0=b0, bw=bw, bk=bk: e.activation(rstd[:, b0:b0 + bw], bank(bk)[:, 0:bw], AF.Sqrt,
                                                                        bias=epst[:, 0:1], scale=1.0 / D),
                     reads=[bbank[bk], bconst], writes=[brstd])
            P.op("dve", lambda e: e.reciprocal(rstd[:, 0:W], rstd[:, 0:W]), reads=[brstd], writes=[brstd])

        def norm_apply(c0, W, gcol, out_t, oc0, bout):
            for dc in range(NDC):
                P.op("dve", lambda e, dc=dc: e.scalar_tensor_tensor(
                    out=out_t[:, dc, oc0:oc0 + W], in0=X[:, dc, c0:c0 + W], scalar=pp[:, gcol + dc:gcol + dc + 1],
                    in1=rstd[:, 0:W], op0=ALU.mult, op1=ALU.mult),
                    reads=[bX[dc], brstd, bpp], writes=[bout[dc]])

        def ffn(c0, T, l):
            drain(ffn_gen(c0, T, l))

        def ffn_gen(c0, T, l):
            W = T + 2
            norm_stats(c0, W)
            norm_apply(c0, W, O_FFNN + 16 * l, U, 0, bU)
            P.op("dve", lambda e: e.tensor_copy(Ue[:, :, :], U[:, :, 0:W:W - 1]), reads=bU, writes=[bUe])
            cwc = O_CW + l * 132
            cbc = O_CB + l * 44
            for fg in range(NFG):
                for j in range(FGS):
                    fc = fg * FGS + j
                    sg = load_panel(s_wup[l, fc], bp_up[l][fc])
                    sv = load_panel(s_wup[l, NFC + fc], bp_up[l][NFC + fc])
                    bg, bv, ek = bank_ring.next(), bank_ring.next(), edge_ring.next()

                    eap = edge_ap(ek)

                    def mm_g(e, sg=sg, bg=bg, eap=eap):
                        for dc in range(NDC):
                            e.matmul(bank(bg)[:, 0:T], wsl[sg][:, dc, :], U[:, dc, 1:1 + T], start=(dc == 0), stop=(dc == 15))
                        r = None
                        for dc in range(NDC):
                            r = e.matmul(eap, wsl[sg][:, dc, :], Ue[:, dc, :], start=(dc == 0), stop=(dc == 15))
                        return r
                    P.op("pe", mm_g, reads=[bws[sg], bUe] + bU, writes=[bbank[bg], bedge[ek]])

                    def mm_v(e, sv=sv, bv=bv):
                        r = None
                        for dc in range(NDC):
                            r = e.matmul(bank(bv)[:, 0:T], wsl[sv][:, dc, :], U[:, dc, 1:1 + T], start=(dc == 0), stop=(dc == 15))
                        return r
                    P.op("pe", mm_v, reads=[bws[sv]] + bU, writes=[bbank[bv]])
                    g, c, q = r2["gsb"].next(), r2["cb"].next(), r2["ge"].next()

                    def f_copy(e, g=g, bg=bg, eap=eap):
                        e.activation(gsb[g][:, 1:1 + T], bank(bg)[:, 0:T], AF.Identity)
                        return e.activation(gsb[g][:, 0:W:W - 1], eap, AF.Identity)
                    P.op("act", f_copy, reads=[bbank[bg], bedge[ek]], writes=[bgsb[g]])
                    P.op("act", lambda e, c=c, bg=bg, fc=fc: e.activation(
                        cb_[c][:, 0:T], bank(bg)[:, 0:T], AF.Identity, bias=pp[:, cbc + fc:cbc + fc + 1],
                        scale=pp[:, cwc + 44 + fc:cwc + 44 + fc + 1]),
                        reads=[bbank[bg], bpp], writes=[bcb[c]])

                    P.op("dve", lambda e, g=g, c=c, fc=fc: e.scalar_tensor_tensor(
                        out=cb_[c][:, 0:T], in0=gsb[g][:, 0:T], scalar=pp[:, cwc + fc:cwc + fc + 1],
                        in1=cb_[c][:, 0:T], op0=ALU.mult, op1=ALU.add), reads=[bgsb[g], bcb[c], bpp], writes=[bcb[c]])
                    P.op("dve", lambda e, g=g, c=c, fc=fc: e.scalar_tensor_tensor(
                        out=cb_[c][:, 0:T], in0=gsb[g][:, 2:2 + T], scalar=pp[:, cwc + 88 + fc:cwc + 88 + fc + 1],
                        in1=cb_[c][:, 0:T], op0=ALU.mult, op1=ALU.add), reads=[bgsb[g], bcb[c], bpp], writes=[bcb[c]])
                    P.op("act", lambda e, c=c, q=q: e.activation(ge_[q][:, 0:T], cb_[c][:, 0:T], AF.Gelu),
                         reads=[bcb[c]], writes=[bge[q]])
                    P.op("dve", lambda e, q=q, bv=bv, j=j: e.tensor_tensor(out=A[:, j, 0:T], in0=ge_[q][:, 0:T],
                                                                          in1=bank(bv)[:, 0:T], op=ALU.mult),
                         reads=[bge[q], bbank[bv]], writes=[bA[j]])
                    yield
                for oc in range(NDC):
                    sd = load_panel(s_wdn[l, oc, fg], bp_dn[l][oc][fg], nk=FGS)
                    bd = bank_ring.next()

                    def mm_d(e, sd=sd, bd=bd):
                        r = None
                        for j in range(FGS):
                            r = e.matmul(bank(bd)[:, 0:T], wsl[sd][:, j, :], A[:, j, 0:T], start=(j == 0), stop=(j == FGS - 1))
                        return r
                    P.op("pe", mm_d, reads=[bws[sd]] + bA, writes=[bbank[bd]])
                    P.op("dve", lambda e, oc=oc, bd=bd: e.tensor_tensor(
                        out=X[:, oc, c0 + 1:c0 + 1 + T], in0=X[:, oc, c0 + 1:c0 + 1 + T], in1=bank(bd)[:, 0:T], op=ALU.add),
                        reads=[bbank[bd], bX[oc]], writes=[bX[oc]])
                    yield

        with ExitStack() as es1:
            def SB1(name, shape, dt):
                return es1.enter_context(nc.sbuf_tensor(name, shape, dt))
            ya = [SB1("ya%d" % i, [128, 4, MAXW], F32) for i in range(3)]
            bya = [Buf() for _ in range(3)]
            dT = [SB1("dT%d" % i, [128, 4, MAXT + 2], BF16) for i in range(2)]
            bdT = [Buf() for _ in range(2)]
            vld = SB1("vld", [128, MAXW], F32)
            bvld = Buf()
            cnt = [SB1("cnt%d" % i, [128, MAXW], F32) for i in range(2)]
            rcnt = SB1("rcnt", [128, 4, MAXT + 2], F32)
            bcnt2 = [Buf(), Buf()]
            brc = [Buf() for _ in range(4)]
            ptmp = SB1("ptmp", [128, MAXT + 2], F32)
            bptmp = Buf()
            qk_st = [SB1("qk%d" % i, [128, MAXT], BF16) for i in range(2)]
            bqk = [Buf() for _ in range(2)]
            vT = [SB1("vT%d" % i, [128, MAXT], BF16) for i in range(2)]
            bvT = [Buf() for _ in range(2)]
            vtok = [SB1("vtok%d" % i, [128, 4, 512], BF16) for i in range(2)]
            bvtok = [Buf() for _ in range(2)]
            Etmp = [SB1("Etmp%d" % i, [128, 12, 128], F32) for i in range(2)]
            bEtmp = [Buf() for _ in range(2)]
            cmask = SB1("cmask", [128, 128], F32)
            bcm = Buf()

            def build_etables(heads):
                if heads and heads[0] == 0:
                    P.op("sp", lambda e: e.dma_start(out=cmask[:, :], in_=cmin), writes=[bcm], dma=True)
                for h in heads:
                    t = r2["Etmp"].next()
                    for kr in range(2):
                        for qr in range(2):
                            o_first = -6 + kr - qr
                            src = bass.AP(rpbF_h, (h * 15 + o_first + 7) * 64 * 32 + 15, [[31, 64], [2 * 64 * 32, 7], [1, 64]])
                            P.op("sp", lambda e, t=t, kr=kr, qr=qr, src=src: e.dma_start(
                                out=Etmp[t][kr * 64:kr * 64 + 64, 0:7, qr * 64:qr * 64 + 64], in_=src),
                                writes=[bEtmp[t]], dma=True)
                    P.op("act", lambda e, t=t: e.activation(Etmp[t][:, 0:7, :], Etmp[t][:, 0:7, :], AF.Exp),
                         reads=[bEtmp[t]], writes=[bEtmp[t]])

                    P.op("dve", lambda e, t=t: e.tensor_tensor(out=Etmp[t][:, 0:7, :], in0=Etmp[t][:, 0:7, :],
                                                               in1=cmask[:, :].unsqueeze(1).to_broadcast([128, 7, 128]), op=ALU.mult),
                         reads=[bEtmp[t], bcm], writes=[bEtmp[t]])
                    P.op("dve", lambda e, t=t: e.tensor_copy(Etmp[t][:, 7:12, :], Etmp[t][:, 1:6, :]),
                         reads=[bEtmp[t]], writes=[bEtmp[t]])

                    def f_e(e, t=t):
                        e.memset(Etmp[t][0:64, 7, 64:128], 0.0)
                        e.memset(Etmp[t][64:128, 11, :], 0.0)
                        return e.memset(Etmp[t][0:64, 11, 0:64], 0.0)
                    P.op("dve", f_e, reads=[bEtmp[t]], writes=[bEtmp[t]])
                    P.op("sp", lambda e, t=t, h=h: e.dma_start(out=s_E[h].rearrange("p (i q) -> p i q", q=128), in_=Etmp[t][:, :, :]),
                         reads=[bEtmp[t]], dma=True)


            etab_done = [0]
            rest = []
            rest.append(conv_wo)
            for fg in range(NFG):
                fcs = list(range(fg * FGS, (fg + 1) * FGS))
                rest.append(lambda fcs=fcs: conv_up(1, fcs + [NFC + f for f in fcs]))
                rest.append(lambda fg=fg: conv_dn(1, [fg]))

            prefetched = {}
            for t_, b_ in ((cnt[0], bcnt2[0]), (cnt[1], bcnt2[1]), (ya[0], bya[0]), (ya[1], bya[1]), (ya[2], bya[2])):
                P.op("pool", lambda e, t_=t_: e.memset(t_[tuple(slice(None) for _ in t_.shape)], 0.0), writes=[b_])
            def tile_geom(ti):
                lr0, nr = P1_TILES[ti]
                T = nr * GW
                return T, T + 2 * XH, lr0 * GW, P1_OFF[ti], T + 2, XH - 1

            def gen_in(ti):
                T, W, lt0, xoff, PW, pc0 = tile_geom(ti)
                P.op("sp", lambda e, xoff=xoff, W=W: e.dma_start(out=vld[:, 0:W], in_=vldin[:, xoff:xoff + W]),
                     writes=[bvld], dma=True)
                for bi, (b0, nb) in enumerate(inblocks(W)):
                    if (ti, bi) in prefetched:
                        ts = prefetched[(ti, bi)]
                    else:
                        ts = r2["tin"].next()
                        P.op("sp", lambda e, ts=ts, b0=b0, nb=nb, xoff=xoff: e.dma_start(
                            out=tin[ts][0:nb, :], in_=xin[xoff + b0:xoff + b0 + nb, :]), writes=[btin[ts]], dma=True)
                    for q in range(4):
                        bk = bank_ring.next()

                        def f_tr(e, ts=ts, nb=nb, q=q, bk=bk):
                            r = None
                            for j in range(4):
                                dc = 4 * q + j
                                r = e.transpose(bank(bk)[:, j * 128:j * 128 + nb], tin[ts][0:nb, dc * 128:(dc + 1) * 128],
                                                ident[0:nb, 0:nb])
                            return r
                        P.op("pe", f_tr, reads=[btin[ts], bconst], writes=[bbank[bk]])
                        src = bank(bk).rearrange("p (j t) -> p j t", t=128)[:, :, 0:nb]
                        if evac_rr.next() == "act":
                            P.op("act", lambda e, q=q, b0=b0, nb=nb, src=src: e.activation(
                                X[:, 4 * q:4 * q + 4, b0:b0 + nb], src, AF.Identity),
                                reads=[bbank[bk]], writes=bX[4 * q:4 * q + 4])
                        else:
                            P.op("dve", lambda e, q=q, b0=b0, nb=nb, src=src: e.tensor_copy(
                                X[:, 4 * q:4 * q + 4, b0:b0 + nb], src),
                                reads=[bbank[bk]], writes=bX[4 * q:4 * q + 4])
                yield
                norm_stats(0, W)

                src_t, src_b = vld, bvld
                for gi in range(4):
                    dst_t, dst_b = cnt[gi % 2], bcnt2[gi % 2]
                    sh = (1 << gi) >> 1
                    if gi == 0:
                        P.op("dve", lambda e, dst_t=dst_t, src_t=src_t, W=W: e.tensor_tensor(
                            out=dst_t[:, 1:W], in0=src_t[:, 0:W - 1], in1=src_t[:, 1:W], op=ALU.add),
                            reads=[src_b], writes=[dst_b])
                    else:
                        P.op("dve", lambda e, dst_t=dst_t, src_t=src_t, W=W, sh=sh: e.tensor_tensor(
                            out=dst_t[:, sh:W - sh], in0=src_t[:, 0:W - 2 * sh], in1=src_t[:, 2 * sh:W], op=ALU.add),
                            reads=[src_b], writes=[dst_b])
                    P.op("dve", lambda e, gi=gi, dst_t=dst_t, PW=PW, pc0=pc0: e.tensor_scalar(
                        out=rcnt[:, gi, 0:PW], in0=dst_t[:, pc0:pc0 + PW], scalar1=1.0, scalar2=None, op0=ALU.max),
                        reads=[dst_b], writes=[brc[gi]])
                    P.op("dve", lambda e, gi=gi, PW=PW: e.reciprocal(rcnt[:, gi, 0:PW], rcnt[:, gi, 0:PW]),
                         reads=[brc[gi]], writes=[brc[gi]])
                    src_t, src_b = dst_t, dst_b
                yield
                for g in range(4):
                    y0, y1, y2 = ya[0], ya[1], ya[2]

                    P.op("dve", lambda e, g=g, W=W: e.tensor_tensor(
                        out=y0[:, :, 0:W], in0=X[:, 4 * g:4 * g + 4, 0:W],
                        in1=rstd[:, 0:W].unsqueeze(1).to_broadcast([128, 4, W]), op=ALU.mult),
                        reads=bX[4 * g:4 * g + 4] + [brstd], writes=[bya[0]])
                    P.op("dve", lambda e, W=W: e.tensor_tensor(out=y1[:, :, 1:W], in0=y0[:, :, 0:W - 1], in1=y0[:, :, 1:W], op=ALU.add),
                         reads=[bya[0]], writes=[bya[1]])
                    si, di = 1, 2
                    for gi in range(1, g + 1):
                        sh = 1 << (gi - 1)
                        P.op("dve", lambda e, si=si, di=di, sh=sh, W=W: e.tensor_tensor(
                            out=ya[di][:, :, sh:W - sh], in0=ya[si][:, :, 0:W - 2 * sh], in1=ya[si][:, :, 2 * sh:W], op=ALU.add),
                            reads=[bya[si]], writes=[bya[di]])
                        si, di = di, si
                    res_buf, bres = ya[di], bya[di]
                    P.op("dve", lambda e, si=si, di=di, g=g, PW=PW, pc0=pc0: e.tensor_tensor(
                        out=ya[di][:, :, 0:PW], in0=ya[si][:, :, pc0:pc0 + PW],
                        in1=rcnt[:, g, 0:PW].unsqueeze(1).to_broadcast([128, 4, PW]), op=ALU.mult),
                        reads=[bya[si], brc[g]], writes=[bya[di]])
                    P.op("dve", lambda e, di=di, PW=PW, pc0=pc0: e.tensor_tensor(
                        out=ya[di][:, :, 0:PW], in0=ya[di][:, :, 0:PW], in1=y0[:, :, pc0:pc0 + PW], op=ALU.subtract),
                        reads=[bya[di], bya[0]], writes=[bya[di]])
                    dsl = r2["dT"].next()
                    for j in range(4):
                        P.op("dve", lambda e, j=j, g=g, dsl=dsl, res_buf=res_buf, PW=PW: e.tensor_scalar(
                            out=dT[dsl][:, j, 0:PW], in0=res_buf[:, j, 0:PW], scalar1=pp[:, O_MIXN + 4 * g + j:O_MIXN + 4 * g + j + 1],
                            scalar2=None, op0=ALU.mult), reads=[bres, bpp], writes=[bdT[dsl]])
                    yield
                    for oc in range(4):
                        s = load_panel(s_wpool[g * 4 + oc], bp_pool[g * 4 + oc], nk=4)
                        blks = colblocks(PW)
                        bks = [bank_ring.next() for _ in blks]

                        def mm_p(e, s=s, dsl=dsl, blks=blks, bks=bks):
                            r = None
                            for kc in range(4):
                                for (b0, bw), bk in zip(blks, bks):
                                    r = e.matmul(bank(bk)[:, 0:bw], wsl[s][:, kc, :], dT[dsl][:, kc, b0:b0 + bw],
                                                 start=(kc == 0), stop=(kc == 3))
                            return r
                        P.op("pe", mm_p, reads=[bws[s], bdT[dsl]], writes=[bbank[k] for k in bks])
                        dc = 4 * g + oc
                        for (b0, bw), bk in zip(blks, bks):
                            P.op("act", lambda e, b0=b0, bw=bw, bk=bk, dc=dc: e.activation(
                                ptmp[:, b0:b0 + bw], bank(bk)[:, 0:bw], AF.Identity, bias=pp[:, O_PBS + dc:O_PBS + dc + 1],
                                scale=pp[:, O_PS + dc:O_PS + dc + 1]), reads=[bbank[bk], bpp], writes=[bptmp])
                        P.op("dve", lambda e, dc=dc, PW=PW, pc0=pc0: e.tensor_tensor(
                            out=X[:, dc, pc0:pc0 + PW], in0=X[:, dc, pc0:pc0 + PW], in1=ptmp[:, 0:PW], op=ALU.add),
                            reads=[bptmp, bX[dc]], writes=[bX[dc]])
                    yield

            def stage_mid(ti):
                T, W, lt0, xoff, PW, pc0 = tile_geom(ti)

                def f_ez(e, pc0=pc0, PW=PW):
                    e.tensor_scalar(out=X[:, :, pc0:pc0 + 1], in0=X[:, :, pc0:pc0 + 1], scalar1=vld[:, pc0:pc0 + 1],
                                    scalar2=None, op0=ALU.mult)
                    return e.tensor_scalar(out=X[:, :, pc0 + PW - 1:pc0 + PW], in0=X[:, :, pc0 + PW - 1:pc0 + PW],
                                           scalar1=vld[:, pc0 + PW - 1:pc0 + PW], scalar2=None, op0=ALU.mult)
                P.op("dve", f_ez, reads=bX + [bvld], writes=bX)
                ffn(pc0, T, 0)
                P.op("act", lambda e, lt0=lt0, T=T: e.dma_start(
                    out=s_x1T.rearrange("(dc p) t -> p dc t", p=128)[:, :, lt0:lt0 + T], in_=X[:, :, XH:XH + T]),
                    reads=bX, dma=True)
                if p1_sel is None and ti + 1 < len(P1_TILES):
                    nW = P1_TILES[ti + 1][1] * GW + 2 * XH
                    nxo = P1_OFF[ti + 1]
                    for bi, (b0, nb) in enumerate(inblocks(nW)[:2]):
                        ts = r2["tin"].next()
                        prefetched[(ti + 1, bi)] = ts
                        P.op("sp", lambda e, ts=ts, b0=b0, nb=nb, nxo=nxo: e.dma_start(
                            out=tin[ts][0:nb, :], in_=xin[nxo + b0:nxo + b0 + nb, :]), writes=[btin[ts]], dma=True)
                norm_stats(XH, T)
                norm_apply(XH, T, O_MIXN + 16, U, 0, bU)

            def gen_qkv(ti):
                T, W, lt0, xoff, PW, pc0 = tile_geom(ti)
                vk = None
                for oc in range(48):
                    s = load_panel(s_wqkv[oc], bp_qkv[oc])
                    bk = bank_ring.next()

                    def mm_q(e, s=s, bk=bk, T=T):
                        r = None
                        for dc in range(NDC):
                            r = e.matmul(bank(bk)[:, 0:T], wsl[s][:, dc, :], U[:, dc, 0:T], start=(dc == 0), stop=(dc == 15))
                        return r
                    P.op("pe", mm_q, reads=[bws[s]] + bU, writes=[bbank[bk]])
                    if oc < 32:
                        st = r2["qk"].next()
                        if oc < 16:
                            bias, scl, dst = pp[:, O_BQS + oc:O_BQS + oc + 1], float(DH ** -0.5), s_QT
                        else:
                            bias, scl, dst = pp[:, O_BK + oc - 16:O_BK + oc - 15], 1.0, s_KT
                        P.op("act", lambda e, st=st, bk=bk, bias=bias, scl=scl, T=T: e.activation(
                            qk_st[st][:, 0:T], bank(bk)[:, 0:T], AF.Identity, bias=bias, scale=scl),
                            reads=[bbank[bk], bpp], writes=[bqk[st]])
                        h = oc % 16
                        P.op("act", lambda e, st=st, dst=dst, h=h, lt0=lt0, T=T: e.dma_start(
                            out=dst[h * 128:(h + 1) * 128, lt0:lt0 + T], in_=qk_st[st][:, 0:T]), reads=[bqk[st]], dma=True)
                    else:
                        h = oc - 32
                        vs = r2["vT"].next()
                        P.op("act", lambda e, vs=vs, bk=bk, h=h, T=T: e.activation(
                            vT[vs][:, 0:T], bank(bk)[:, 0:T], AF.Identity, bias=pp[:, O_BV + h:O_BV + h + 1], scale=1.0),
                            reads=[bbank[bk], bpp], writes=[bvT[vs]])
                        nblk = T // 128
                        bk2 = bank_ring.next()
                        pbf = bank(bk2).bitcast(BF16)

                        def f_vtr(e, vs=vs, pbf=pbf, nblk=nblk):
                            r = None
                            for b in range(nblk):
                                r = e.transpose(pbf[:, b * 128:(b + 1) * 128], vT[vs][:, b * 128:(b + 1) * 128], identb[:, :])
                            return r
                        P.op("pe", f_vtr, reads=[bvT[vs], bconst], writes=[bbank[bk2]])
                        if h % 4 == 0:
                            vk = r2["vtok"].next()
                        hq = h % 4
                        P.op("dve", lambda e, vk=vk, hq=hq, pbf=pbf, nblk=nblk: e.tensor_copy(
                            vtok[vk][:, 0:nblk, hq * 128:(hq + 1) * 128],
                            pbf[:, 0:nblk * 128].rearrange("p (b c) -> p b c", c=128)),
                            reads=[bbank[bk2]], writes=[bvtok[vk]])
                        if hq == 3:
                            hg = h // 4
                            P.op("act", lambda e, vk=vk, hg=hg, lt0=lt0, T=T, nblk=nblk: e.dma_start(
                                out=s_V[lt0:lt0 + T, hg * 512:(hg + 1) * 512].rearrange("(b p) c -> p b c", p=128),
                                in_=vtok[vk][:, 0:nblk, :]), reads=[bvtok[vk]], dma=True)
                    yield

            tiles = [ti for ti in range(len(P1_TILES)) if p1_sel is None or ti in p1_sel]
            drain(gen_in(tiles[0]))
            stage_mid(tiles[0])
            for k, ti in enumerate(tiles):
                gq = gen_qkv(ti)
                if k + 1 < len(tiles):
                    ga = gen_in(tiles[k + 1])
                    for step in (("q", 2), ("a", 1), ("q", 4), ("a", 1), ("q", 4), ("a", 1), ("q", 6), ("a", 2),
                                 ("q", 6), ("a", 2), ("q", 7), ("a", 2), ("q", 7), ("a", 1)):
                        drain(gq if step[0] == "q" else ga, step[1])
                    drain(gq)
                    drain(ga)
                    stage_mid(tiles[k + 1])
                else:
                    drain(gq)
                if rest:
                    rest.pop(0)()
                if k >= 1 and etab_done[0] < NH:
                    build_etables(list(range(etab_done[0], etab_done[0] + 2)))
                    etab_done[0] += 2
            while rest:
                rest.pop(0)()
            if etab_done[0] < NH:
                build_etables(list(range(etab_done[0], NH)))
            P.barrier()
        with ExitStack() as es2:
            def SB2(name, shape, dt):
                return es2.enter_context(nc.sbuf_tensor(name, shape, dt))
            Qb = SB2("Qb", [128, NH, MAXT + 2], BF16)
            bQb = Buf()
            Kb = [SB2("Kb%d" % i, [128, 2, 1152], BF16) for i in range(2)]
            bKb = [Buf() for _ in range(2)]
            Vb = [SB2("Vb%d" % i, [128, 9, 256], BF16) for i in range(2)]
            bVb = [Buf() for _ in range(2)]
            Eb = [SB2("Eb%d" % i, [128, 12, 128], F32) for i in range(2)]
            bEb = [Buf() for _ in range(2)]
            pexp = [SB2("pexp%d" % i, [128, 6, 128], F32) for i in range(2)]
            bpexp = [Buf() for _ in range(2)]
            PT = [SB2("PT%d" % i, [128, 6, 128], BF16) for i in range(2)]
            bPT = [Buf() for _ in range(2)]
            rden = [SB2("rden%d" % i, [128, 128], F32) for i in range(2)]
            brden = [Buf() for _ in range(2)]
            mk = SB2("mk", [128, 12, 6, 2], F32)
            ev = SB2("ev", [128, 2 * N_P2], F32)
            bmk = Buf()
            P.op("sp", lambda e: e.dma_start(out=mk[:, :, :, :], in_=mkin.rearrange("p (a b c) -> p a b c", b=6, c=2)),
                 writes=[bmk], dma=True)
            P.op("sp", lambda e: e.dma_start(out=ev[:, :], in_=evin), writes=[bmk], dma=True)
            OT = SB2("OT", [128, NH, MAXT + 2], BF16)
            bOT = [Buf() for _ in range(NH)]
            bod = [Buf(), Buf()]
            bank_ring.items = [4, 5, 6]
            out_stores = []

            pendB = [None]

            def attn(h, hh, ks, vs, es_, chunks, e0, qa, nq, q_ap, o_ap, mk_ap):
                n = len(chunks)
                sp_ = 0
                S = psT[sp_]

                def mm_s(e):
                    r = None
                    for t, j in enumerate(chunks):
                        r = e.matmul(S[:, t * 128:t * 128 + nq], Kb[ks][:, hh, j * 128:(j + 1) * 128], q_ap, start=True, stop=True)
                    return r
                P.op("pe", mm_s, reads=[bKb[ks], bQb], writes=[bbank[2 * sp_], bbank[2 * sp_ + 1]])
                px, pt, rd = r2["pexp"].next(), r2["PT"].next(), r2["rden"].next()

                def f_exp(e):
                    return e.activation(pexp[px][:, 0:n, 0:nq], S[:, 0:n * 128].rearrange("p (t q) -> p t q", q=128)[:, :, 0:nq], AF.Exp)
                P.op("act", f_exp, reads=[bbank[2 * sp_], bbank[2 * sp_ + 1]], writes=[bpexp[px]])

                P.op("dve", lambda e: e.tensor_tensor(out=PT[pt][:, 0:n, 0:nq], in0=pexp[px][:, 0:n, 0:nq],
                                                      in1=Eb[es_][:, e0:e0 + n, qa:qa + nq], op=ALU.mult),
                     reads=[bpexp[px], bEb[es_]], writes=[bPT[pt]])
                if mk_ap is not None:
                    P.op("dve", lambda e: e.tensor_tensor(out=PT[pt][:, 0:n, :].rearrange("p t (r c) -> p t r c", r=2),
                                                          in0=PT[pt][:, 0:n, :].rearrange("p t (r c) -> p t r c", r=2),
                                                          in1=mk_ap.unsqueeze(3).to_broadcast([128, n, 2, 64]), op=ALU.mult),
                         reads=[bPT[pt], bmk], writes=[bPT[pt]])

                def partB():
                    ob = 2 + r2["od"].next()
                    ocol = 0
                    bo_ = bbank[ob]

                    def mm_o(e):
                        for t, j in enumerate(chunks):
                            e.matmul(bank(ob)[:, ocol:ocol + nq], Vb[vs][:, j, hh * 128:(hh + 1) * 128], PT[pt][:, t, 0:nq],
                                     start=(t == 0), stop=(t == n - 1))
                        r = None
                        for t in range(n):
                            r = e.matmul(bank(ob)[:, ocol + 128:ocol + 128 + nq], ones[:, :], PT[pt][:, t, 0:nq], start=(t == 0), stop=(t == n - 1))
                        return r
                    P.op("pe", mm_o, reads=[bVb[vs], bPT[pt], bconst], writes=[bo_])
                    P.op("act", lambda e: e.activation(rden[rd][:, 0:nq], bank(ob)[:, ocol + 128:ocol + 128 + nq], AF.Ln),
                         reads=[bo_], writes=[brden[rd]])
                    P.op("act", lambda e: e.activation(rden[rd][:, 0:nq], rden[rd][:, 0:nq], AF.Exp, scale=-1.0),
                         reads=[brden[rd]], writes=[brden[rd]])
                    P.op("dve", lambda e: e.tensor_tensor(out=o_ap, in0=bank(ob)[:, ocol:ocol + nq], in1=rden[rd][:, 0:nq], op=ALU.mult),
                         reads=[bo_, brden[rd]], writes=[bOT[h]])
                if pendB[0] is not None:
                    pendB[0]()
                pendB[0] = partB

            def attn_flush():
                if pendB[0] is not None:
                    pendB[0]()
                    pendB[0] = None

            T = MAXT

            def gen_attn(i):
                q0 = HALO_TOP + 8 * i
                tq0 = q0 * GW
                wt0 = (q0 - 6) * GW
                P.op("sp", lambda e, tq0=tq0: e.dma_start(
                    out=Qb[:, :, 0:T + 2], in_=s_QT.rearrange("(h p) t -> p h t", p=128)[:, :, tq0 - 1:tq0 + T + 1]),
                    writes=[bQb], dma=True)
                gmod = (8 * i) % 32
                for hg in range(8):
                    ks, vs = r2["Kb"].next(), r2["Vb"].next()
                    P.op("sp", lambda e, ks=ks, hg=hg, wt0=wt0: e.dma_start(
                        out=Kb[ks][:, :, :], in_=s_KT[hg * 256:(hg + 1) * 256, wt0:wt0 + 1152].rearrange("(h p) t -> p h t", p=128)),
                        writes=[bKb[ks]], dma=True)
                    P.op("sp", lambda e, vs=vs, hg=hg, wt0=wt0: e.dma_start(
                        out=Vb[vs][:, :, :], in_=s_V[wt0:wt0 + 1152, hg * 256:(hg + 1) * 256].rearrange("(j p) c -> p j c", p=128)),
                        writes=[bVb[vs]], dma=True)
                    for hh in range(2):
                        h = 2 * hg + hh
                        es_ = r2["Eb"].next()
                        P.op("sp", lambda e, es_=es_, h=h: e.dma_start(
                            out=Eb[es_][:, :, :], in_=s_E[h].rearrange("p (i q) -> p i q", q=128)), writes=[bEb[es_]], dma=True)
                        attn(h, hh, ks, vs, es_, [0, 1, 2, 3, 4], 7, 127, 1, Qb[:, h, 0:1], OT[:, h, 0:1], None)
                        yield
                        for k in range(4):
                            mk_ap = None
                            if gmod == 0 and k < 2:
                                chunks, e0 = list(range(k + 1, k + 7)), 1
                                mk_ap = mk[:, (i // 4) * 4 + k, :, :]
                            elif gmod == 24 and k >= 2:
                                chunks, e0 = list(range(k, k + 6)), 0
                                mk_ap = mk[:, (i // 4) * 4 + k, :, :]
                            else:
                                chunks, e0 = list(range(k + 1, k + 6)), 7
                            attn(h, hh, ks, vs, es_, chunks, e0, 0, 128, Qb[:, h, 1 + 128 * k:1 + 128 * k + 128],
                                 OT[:, h, 1 + 128 * k:1 + 128 * k + 128], mk_ap)
                            yield
                        attn(h, hh, ks, vs, es_, [5, 6, 7, 8], 7, 0, 1, Qb[:, h, T + 1:T + 2], OT[:, h, T + 1:T + 2], None)
                        yield
                attn_flush()

            def wo_stage(i):
                tq0 = (HALO_TOP + 8 * i) * GW
                P.op("sp", lambda e, tq0=tq0: e.dma_start(
                    out=X[:, :, 0:T + 2], in_=s_x1T.rearrange("(dc p) t -> p dc t", p=128)[:, :, tq0 - 1:tq0 + T + 1]),
                    writes=bX, dma=True)
                for oc in range(NDC):
                    s = load_panel(s_wo[oc], bp_wo[oc])
                    bk, ek = bank_ring.next(), edge_ring.next()

                    eap = edge_ap(ek)

                    def mm_wo(e, s=s, bk=bk, eap=eap):
                        for hc in range(NH):
                            e.matmul(bank(bk)[:, 0:T], wsl[s][:, hc, :], OT[:, hc, 1:1 + T], start=(hc == 0), stop=(hc == 15))
                        r = None
                        for hc in range(NH):
                            r = e.matmul(eap, wsl[s][:, hc, :], OT[:, hc, 0:T + 2:T + 1], start=(hc == 0), stop=(hc == 15))
                        return r
                    P.op("pe", mm_wo, reads=[bws[s]] + bOT, writes=[bbank[bk], bedge[ek]])

                    def f_wo(e, oc=oc, bk=bk, eap=eap):
                        e.tensor_tensor(out=X[:, oc, 1:1 + T], in0=X[:, oc, 1:1 + T], in1=bank(bk)[:, 0:T], op=ALU.add)
                        return e.tensor_tensor(out=X[:, oc, 0:T + 2:T + 1], in0=X[:, oc, 0:T + 2:T + 1], in1=eap, op=ALU.add)
                    P.op("dve", f_wo, reads=[bbank[bk], bedge[ek], bX[oc]], writes=[bX[oc]])

                def f_ez2(e, i=i):
                    e.tensor_scalar(out=X[:, :, 0:1], in0=X[:, :, 0:1], scalar1=ev[:, 2 * i:2 * i + 1], scalar2=None, op0=ALU.mult)
                    return e.tensor_scalar(out=X[:, :, T + 1:T + 2], in0=X[:, :, T + 1:T + 2], scalar1=ev[:, 2 * i + 1:2 * i + 2],
                                           scalar2=None, op0=ALU.mult)
                P.op("dve", f_ez2, reads=bX + [bmk], writes=bX)
                if debug:
                    out_stores.append(P.op("pool", lambda e, i=i: e.dma_start(out=dbg_x[i], in_=X[:, :, 0:T + 2]), reads=bX, dma=True))
                    out_stores.append(P.op("pool", lambda e, i=i: e.dma_start(out=dbg_o[i], in_=OT[:, :, 0:T + 2]), reads=bOT, dma=True))

            def tail_stage(i):
                norm_stats(1, T)
                norm_apply(1, T, O_FIN, X, 1, bX)
                for tb in range(T // 128):
                    ts = r2["tin"].next()
                    for q in range(4):
                        bk = bank_ring.next()

                        def f_tro(e, tb=tb, q=q, bk=bk):
                            r = None
                            for j in range(4):
                                dc = 4 * q + j
                                r = e.transpose(bank(bk)[:, j * 128:(j + 1) * 128], X[:, dc, 1 + tb * 128:1 + (tb + 1) * 128], ident[:, :])
                            return r
                        P.op("pe", f_tro, reads=bX[4 * q:4 * q + 4] + [bconst], writes=[bbank[bk]])
                        if evac_rr.next() == "act":
                            P.op("act", lambda e, ts=ts, q=q, bk=bk: e.activation(tin[ts][:, q * 512:(q + 1) * 512], bank(bk), AF.Identity),
                                 reads=[bbank[bk]], writes=[btin[ts]])
                        else:
                            P.op("dve", lambda e, ts=ts, q=q, bk=bk: e.tensor_copy(tin[ts][:, q * 512:(q + 1) * 512], bank(bk)),
                                 reads=[bbank[bk]], writes=[btin[ts]])
                    r0 = i * 512 + tb * 128
                    out_stores.append(P.op("pool", lambda e, ts=ts, r0=r0: e.dma_start(out=yout[r0:r0 + 128, :], in_=tin[ts][:, :]),
                                           reads=[btin[ts]], dma=True))
            tiles2 = [i for i in range(N_P2) if p2_sel is None or i in p2_sel]
            drain(gen_attn(tiles2[0]))
            for k, i in enumerate(tiles2):
                wo_stage(i)
                gf = ffn_gen(0, T, 1)
                if k + 1 < len(tiles2):
                    ga = gen_attn(tiles2[k + 1])
                    fa, aa = True, True
                    while fa or aa:
                        if fa:
                            fa = drain(gf, 1)
                        if aa:
                            aa = drain(ga, 1)
                else:
                    drain(gf)
                tail_stage(i)
            P.emit(final_wait_ops=out_stores)
    return nc


def _seq_of_row(g):
    for (a, b) in SEQS:
        if a <= g < b:
            return (a, b)
    return None


def _prep_core(c, Xall, pp, consts, weights):
    base_row = ROWS_CORE * c - HALO_TOP
    xin = np.zeros((N_EXT, D), np.float32)
    vld = np.zeros((N_EXT,), np.float32)
    for ti, (lr0, nr) in enumerate(P1_TILES):
        g0 = base_row + lr0
        sq = _seq_of_row(g0)
        if sq is None:
            continue
        t_lo, t_hi = sq[0] * GW, sq[1] * GW
        e0 = g0 * GW - XH
        n = nr * GW + 2 * XH
        a = max(e0, t_lo)
        b = min(e0 + n, t_hi)
        if b > a:
            xin[P1_OFF[ti] + (a - e0):P1_OFF[ti] + (b - e0)] = Xall[a:b]
            vld[P1_OFF[ti] + (a - e0):P1_OFF[ti] + (b - e0)] = 1.0
    ev = np.zeros((2 * N_P2,), np.float32)
    mk = np.zeros((128, 12, 6, 2), np.float32)
    for i in range(N_P2):
        gq0 = ROWS_CORE * c + 8 * i
        sq = _seq_of_row(gq0)
        ev[2 * i] = 1.0 if gq0 - 1 >= sq[0] else 0.0
        ev[2 * i + 1] = 1.0 if gq0 + 8 < sq[1] else 0.0
        gmod = (8 * i) % 32
        for k in range(4):
            if gmod == 0 and k < 2:
                ws = gq0 + 2 * k - 4
            elif gmod == 24 and k >= 2:
                ws = gq0 + 2 * k - 6
            else:
                continue
            pe_i = (i // 4) * 4 + k
            for qr in range(2):
                qg = gq0 + 2 * k + qr
                s0, s1 = _seq_of_row(qg)
                rs = min(max(qg - 4, s0), s1 - 8)
                for n_ in range(12):
                    kg = ws + n_
                    m = 1.0 if rs <= kg < rs + 8 else 0.0
                    j, kr = n_ // 2, n_ % 2
                    mk[kr * 64:(kr + 1) * 64, pe_i, j, qr] = m
    d = {
        "xin": xin,
        "vldin": np.ascontiguousarray(np.broadcast_to(vld[None, :], (128, N_EXT))),
        "evin": np.ascontiguousarray(np.broadcast_to(ev[None, :], (128, 2 * N_P2))),
        "mkin": np.ascontiguousarray(mk.reshape(128, 144)),
        "ppin": pp,
    }
    d.update(consts)
    d.update(weights)
    return d


def _fm(v):
    v = np.asarray(v, np.float32)
    return np.ascontiguousarray(v.reshape(-1, 128).T)


_NC_CACHE = {}


def kernel(x_prompt, x_sample, mix_norm, pool_w, pool_b, pool_scale, attn_w_qkv, attn_b_qkv, attn_rpb, attn_w_o,
           ffn_norm, ffn_w_up, ffn_conv_w, ffn_conv_b, ffn_w_down, final_norm, _debug=False, _p1=None, _p2=None):
    f32 = lambda a: np.ascontiguousarray(np.asarray(a, dtype=np.float32))
    x_prompt, x_sample = f32(x_prompt), f32(x_sample)
    Xall = np.concatenate([x_prompt.reshape(-1, D), x_sample.reshape(-1, D)], axis=0)
    pp = np.zeros((128, NPP), np.float32)
    mix_norm, ffn_norm, final_norm = f32(mix_norm), f32(ffn_norm), f32(final_norm)
    pp[:, O_MIXN:O_MIXN + 16] = _fm(mix_norm[0])
    pp[:, O_MIXN + 16:O_MIXN + 32] = _fm(mix_norm[1])
    pp[:, O_FFNN:O_FFNN + 16] = _fm(ffn_norm[0])
    pp[:, O_FFNN + 16:O_FFNN + 32] = _fm(ffn_norm[1])
    pp[:, O_FIN:O_FIN + 16] = _fm(final_norm)
    pp[:, O_PB:O_PB + 16] = _fm(f32(pool_b).reshape(-1))
    pp[:, O_PS:O_PS + 16] = _fm(f32(pool_scale).reshape(-1))
    cw, cb = f32(ffn_conv_w), f32(ffn_conv_b)
    for l in range(2):
        for k in range(3):
            pp[:, O_CW + l * 132 + k * 44:O_CW + l * 132 + (k + 1) * 44] = _fm(cw[l, k])
        pp[:, O_CB + l * 44:O_CB + (l + 1) * 44] = _fm(cb[l])
    bqkv = f32(attn_b_qkv).reshape(-1)
    pp[:, O_BQ:O_BQ + 16] = _fm(bqkv[0:D])
    pp[:, O_BK:O_BK + 16] = _fm(bqkv[D:2 * D])
    pp[:, O_BV:O_BV + 16] = _fm(bqkv[2 * D:3 * D])
    rpb = f32(attn_rpb).reshape(NH, 15, 31)
    rf = np.zeros((NH, 15, 64, 32), np.float32)
    rf[:, :, :, 0:31] = rpb[:, :, None, ::-1]
    cq = np.arange(GW)
    cs = np.clip(cq - 8, 0, GW - 16)
    cm = ((cq[:, None] >= cs[None, :]) & (cq[:, None] < cs[None, :] + 16)).astype(np.float32)
    consts = {"rpbF": rf.reshape(-1), "cmin": np.ascontiguousarray(np.tile(cm, (2, 2))),
              "idin": np.eye(128, dtype=np.float32)}
    weights = {"w_pool": f32(pool_w).reshape(4, 512, 512), "w_qkv": f32(attn_w_qkv).reshape(D, 3 * D),
               "w_o": f32(attn_w_o).reshape(D, D), "w_up": f32(ffn_w_up), "w_dn": f32(ffn_w_down)}
    in_maps = [_prep_core(c, Xall, pp, consts, weights) for c in range(N_CORES)]
    key = (bool(_debug), str(_p1), str(_p2))
    if key not in _NC_CACHE:
        _NC_CACHE[key] = build_program(debug=_debug, p1_sel=_p1, p2_sel=_p2)
    nc = _NC_CACHE[key]
    res = run_bass_kernel_spmd(nc, in_maps, core_ids=list(range(N_CORES)))
    if _debug:
        return res
    Y = np.concatenate([np.asarray(r["yout"], dtype=np.float32) for r in res.results], axis=0)
    y_prompt = Y[0:16384].reshape(1, 16384, D)
    y_sample = Y[16384:].reshape(4, 8192, D)
    return (np.ascontiguousarray(y_prompt), np.ascontiguousarray(y_sample))
```

```python
import numpy as np
import ml_dtypes
from contextlib import ExitStack
import concourse.bass as bass
import concourse.mybir as mybir
from concourse.bass_utils import run_bass_kernel_spmd

F32 = mybir.dt.float32
BF16 = mybir.dt.bfloat16
AF = mybir.ActivationFunctionType
ALU = mybir.AluOpType

D = 2048
NDC = 16
DFF = 5632
NFC = 44
NFG = 4
FGS = 11
NH = 16
DH = 128
GW = 64
EPS = 1e-6
N_CORES = 8
ROWS_CORE = 96
HALO_TOP = 6
HALO_BOT = 4
NLR = HALO_TOP + ROWS_CORE + HALO_BOT
NLT = NLR * GW
XH = 10
SEQS = [(0, 256), (256, 384), (384, 512), (512, 640), (640, 768)]
TOT_ROWS = 768
POOL_W = (2, 4, 8, 16)

P1_TILES = [(0, HALO_TOP)] + [(HALO_TOP + 8 * k, 8) for k in range(12)] + [(HALO_TOP + 96, HALO_BOT)]
P1_EXT = [r * GW + 2 * XH for (_, r) in P1_TILES]
P1_OFF = [int(x) for x in np.cumsum([0] + P1_EXT[:-1])]
N_EXT = int(sum(P1_EXT))
N_P2 = 12
MAXT = 512
MAXW = MAXT + 2 * XH

O_MIXN, O_FFNN, O_FIN, O_PB, O_PS, O_CW, O_CB, O_BQ, O_BK, O_BV, O_PBS, O_BQS, NPP = \
    0, 32, 64, 80, 96, 112, 376, 464, 480, 496, 512, 528, 544

ENGS = ("pe", "act", "dve", "pool", "sp")
N_DMA_SEMS = 34
DMA_SHARE = {"sp": list(range(0, 16)), "pool": list(range(16, 26)), "act": list(range(26, 34)), "dve": [], "pe": []}


class Buf:
    __slots__ = ("name", "w", "r")

    def __init__(self, name=""):
        self.name = name
        self.w = None
        self.r = []


class Op:
    __slots__ = ("eng", "fn", "deps", "sig", "tok", "dma", "barrier")

    def __init__(self, eng, fn, dma):
        self.eng = eng
        self.fn = fn
        self.deps = []
        self.sig = False
        self.tok = None
        self.dma = dma
        self.barrier = None


class Prog:
    def __init__(self, nc):
        self.nc = nc
        self.ops = {e: [] for e in ENGS}
        self.nbar = 0

    def op(self, eng, fn, reads=(), writes=(), dma=False):
        o = Op(eng, fn, dma)
        deps = {}
        for b in reads:
            if b.w is not None:
                deps[id(b.w)] = b.w
        for b in writes:
            if b.w is not None:
                deps[id(b.w)] = b.w
            for rr in b.r:
                deps[id(rr)] = rr
        for d in deps.values():
            if d.eng == "pe" and eng == "pe" and not d.dma and not dma:
                continue
            d.sig = True
            o.deps.append(d)
        for b in reads:
            b.r.append(o)
        for b in writes:
            b.w = o
            b.r = []
        self.ops[eng].append(o)
        return o

    def barrier(self):
        k = self.nbar
        self.nbar += 1
        for e in ENGS:
            for o in reversed(self.ops[e]):
                if o.barrier is None and not o.dma:
                    o.sig = True
                    break
            m = Op(e, None, False)
            m.barrier = k
            self.ops[e].append(m)

    def emit(self, final_wait_ops=()):
        nc = self.nc
        with ExitStack() as es:
            esem = {e: es.enter_context(nc.semaphore("s_" + e)) for e in ENGS}
            dsems = [es.enter_context(nc.semaphore("d%d" % i)) for i in range(N_DMA_SEMS)]
            ecount = {e: 0 for e in ENGS}
            dcount = [0] * N_DMA_SEMS
            dlast = [None] * N_DMA_SEMS
            drr = {e: 0 for e in ENGS}
            prev_on_sem = {}
            snap_e = {}
            snap_d = {}
            for e in ENGS:
                for o in self.ops[e]:
                    if o.barrier is not None:
                        snap_e.setdefault(o.barrier, {})[e] = ecount[e]
                        sd = snap_d.setdefault(o.barrier, [0] * N_DMA_SEMS)
                        for k in DMA_SHARE[e]:
                            sd[k] = dcount[k]
                    elif o.dma:
                        lst = DMA_SHARE[e]
                        k = lst[drr[e] % len(lst)]
                        drr[e] += 1
                        if dlast[k] is not None:
                            prev_on_sem[id(o)] = dlast[k]
                        dcount[k] += 16
                        o.tok = (k, dcount[k])
                        dlast[k] = o
                    elif o.sig:
                        ecount[e] += 1
                        o.tok = (e, ecount[e])

            def semof(key):
                return esem[key] if isinstance(key, str) else dsems[key]

            block = es.enter_context(nc.Block())
            prog = self

            def run(e, eng):
                known = {}
                for o in prog.ops[e]:
                    if o.barrier is not None:
                        for f, v in snap_e[o.barrier].items():
                            if v > known.get(f, 0):
                                eng.wait_ge(esem[f], v)
                                known[f] = v
                        for k, v in enumerate(snap_d[o.barrier]):
                            if v > known.get(k, 0):
                                eng.wait_ge(dsems[k], v)
                                known[k] = v
                        continue
                    waits = {}
                    dl = o.deps
                    if o.dma and id(o) in prev_on_sem:
                        dl = dl + [prev_on_sem[id(o)]]
                    for d in dl:
                        key, v = d.tok
                        if known.get(key, 0) >= v:
                            continue
                        if waits.get(key, 0) < v:
                            waits[key] = v
                    for key, v in waits.items():
                        eng.wait_ge(semof(key), v)
                        known[key] = v
                    ins = o.fn(eng)
                    if o.tok is not None:
                        key, v = o.tok
                        ins.then_inc(semof(key), 16 if o.dma else 1)
                if e == "sp":
                    for o in final_wait_ops:
                        key, v = o.tok
                        eng.wait_ge(semof(key), v)

            @block.tensor
            def _(eng):
                run("pe", eng)

            @block.scalar
            def _(eng):
                run("act", eng)

            @block.vector
            def _(eng):
                run("dve", eng)

            @block.gpsimd
            def _(eng):
                run("pool", eng)

            @block.sync
            def _(eng):
                run("sp", eng)


def drain(g, n=None):
    k = 0
    while n is None or k < n:
        try:
            next(g)
        except StopIteration:
            return False
        k += 1
    return True


def inblocks(n):
    out, s_ = [], 0
    while s_ < n:
        w = min(128, n - s_)
        out.append((s_, w))
        s_ += w
    return out


def colblocks(n, maxw=512):
    k = -(-n // maxw)
    base, rem = n // k, n % k
    out, s = [], 0
    for i in range(k):
        w = base + (1 if i < rem else 0)
        out.append((s, w))
        s += w
    return out


class Ring:
    def __init__(self, items):
        self.items = items
        self.i = 0

    def next(self):
        it = self.items[self.i % len(self.items)]
        self.i += 1
        return it


def build_program(debug=False, p1_sel=None, p2_sel=None):
    nc = bass.Bass("TRN2", target_bir_lowering=False)
    dt_in = lambda name, shape, dt=F32: nc.dram_tensor(name, shape, dt, kind="ExternalInput")
    skind = dict(kind="ExternalOutput") if debug else {}
    xin = dt_in("xin", [N_EXT, D]).ap()
    vldin = dt_in("vldin", [128, N_EXT]).ap()
    evin = dt_in("evin", [128, 2 * N_P2]).ap()
    mkin = dt_in("mkin", [128, 12 * 12]).ap()
    ppin = dt_in("ppin", [128, NPP]).ap()
    rpbF_h = dt_in("rpbF", [NH * 15 * 64 * 32])
    cmin = dt_in("cmin", [128, 128]).ap()
    idin = dt_in("idin", [128, 128]).ap()
    w_pool = dt_in("w_pool", [4, 512, 512]).ap()
    w_qkv = dt_in("w_qkv", [D, 3 * D]).ap()
    w_o = dt_in("w_o", [D, D]).ap()
    w_up = dt_in("w_up", [2, D, 2 * DFF]).ap()
    w_dn = dt_in("w_dn", [2, DFF, D]).ap()
    yout = nc.dram_tensor("yout", [ROWS_CORE * GW, D], F32, kind="ExternalOutput").ap()
    s_wpool = nc.dram_tensor("s_wpool", [16, 128, 4, 128], BF16).ap()
    s_wup = nc.dram_tensor("s_wup", [2, 88, 128, 16, 128], BF16).ap()
    s_wdn = nc.dram_tensor("s_wdn", [2, 16, NFG, 128, FGS, 128], BF16).ap()
    s_wqkv = nc.dram_tensor("s_wqkv", [48, 128, 16, 128], BF16).ap()
    s_wo = nc.dram_tensor("s_wo", [16, 128, 16, 128], BF16).ap()
    s_x1T = nc.dram_tensor("s_x1T", [D, NLT], F32, **skind).ap()
    s_QT = nc.dram_tensor("s_QT", [D, NLT], BF16, **skind).ap()
    s_KT = nc.dram_tensor("s_KT", [D, NLT], BF16, **skind).ap()
    s_V = nc.dram_tensor("s_V", [NLT, D], BF16, **skind).ap()
    s_E = nc.dram_tensor("s_E", [NH, 128, 12 * 128], F32, **skind).ap()
    if debug:
        dbg_x = nc.dram_tensor("dbg_x", [N_P2, 128, NDC, MAXT + 2], F32, kind="ExternalOutput").ap()
        dbg_o = nc.dram_tensor("dbg_o", [N_P2, 128, NDC, MAXT + 2], BF16, kind="ExternalOutput").ap()

    with ExitStack() as es:
        def SB(name, shape, dt):
            return es.enter_context(nc.sbuf_tensor(name, shape, dt))

        P = Prog(nc)
        X = SB("X", [128, NDC, MAXW], F32)
        U = SB("U", [128, NDC, MAXW], BF16)
        A = SB("A", [128, FGS, MAXT], BF16)
        NWS = 7
        wsl = [SB("wsl%d" % i, [128, 16, 128], BF16) for i in range(NWS)]
        tin = [SB("tin%d" % i, [128, D], F32) for i in range(2)]
        gsb = [SB("gsb%d" % i, [128, MAXT + 2], F32) for i in range(2)]
        cb_ = [SB("cb%d" % i, [128, MAXT], F32) for i in range(2)]
        ge_ = [SB("ge%d" % i, [128, MAXT], F32) for i in range(2)]
        sq_ = [SB("sq%d" % i, [128, 4, MAXW], BF16) for i in range(2)]
        rstd = SB("rstd", [128, MAXW], F32)
        Ue = SB("Ue", [128, NDC, 2], BF16)
        pp = SB("pp", [128, NPP], F32)
        ident = SB("ident", [128, 128], F32)
        identb = SB("identb", [128, 128], BF16)
        ones = SB("ones", [128, 128], BF16)
        epst = SB("epst", [128, 1], F32)
        psT = [es.enter_context(nc.psum_tensor("ps%d" % i, [128, 1024], F32)) for i in range(4)]

        def bank(b):
            return psT[b // 2][:, (b % 2) * 512:(b % 2) * 512 + 512]

        bX = [Buf("X%d" % i) for i in range(NDC)]
        bU = [Buf("U%d" % i) for i in range(NDC)]
        bA = [Buf("A%d" % i) for i in range(FGS)]
        bws = [Buf("ws%d" % i) for i in range(NWS)]
        btin = [Buf() for _ in range(2)]
        bgsb = [Buf() for _ in range(2)]
        bcb = [Buf() for _ in range(2)]
        bge = [Buf() for _ in range(2)]
        bsq = [Buf() for _ in range(2)]
        brstd = Buf("rstd")
        bUe = Buf("Ue")
        bpp = Buf("pp")
        bconst = Buf("const")
        bbank = [Buf("bank%d" % i) for i in range(8)]
        bedge = [Buf("edge%d" % i) for i in range(8)]
        ws_ring = Ring(list(range(NWS)))
        bank_ring = Ring(list(range(7)))
        edge_ring = Ring(list(range(8)))
        r2 = {n: Ring([0, 1]) for n in ("tin", "gsb", "cb", "ge", "sq", "dT", "vT", "vtok", "Etmp", "qk",
                                         "Kb", "Vb", "Eb", "pexp", "PT", "rden", "S", "od")}
        evac_rr = Ring(["act", "dve"])

        edge_loc = [3, 512]

        def edge_ap(k):
            return psT[edge_loc[0]][:, edge_loc[1] + 2 * k: edge_loc[1] + 2 * k + 2]

        P.op("sp", lambda e: e.dma_start(out=pp[:, :], in_=ppin), writes=[bpp], dma=True)
        P.op("sp", lambda e: e.dma_start(out=ident[:, :], in_=idin), writes=[bconst], dma=True)

        P.op("dve", lambda e: e.tensor_copy(identb[:, :], ident[:, :]), reads=[bconst], writes=[bconst])
        P.op("dve", lambda e: e.memset(ones[:, :], 1.0), writes=[bconst])
        P.op("dve", lambda e: e.memset(epst[:, :], EPS), writes=[bconst])
        P.op("dve", lambda e: e.tensor_tensor(out=pp[:, O_PBS:O_PBS + 16], in0=pp[:, O_PB:O_PB + 16],
                                              in1=pp[:, O_PS:O_PS + 16], op=ALU.mult), reads=[bpp], writes=[bpp])
        P.op("dve", lambda e: e.tensor_scalar(out=pp[:, O_BQS:O_BQS + 16], in0=pp[:, O_BQ:O_BQ + 16],
                                              scalar1=float(DH ** -0.5), scalar2=None, op0=ALU.mult), reads=[bpp], writes=[bpp])

        bp_pool = [Buf() for _ in range(16)]
        bp_up = [[Buf() for _ in range(88)] for _ in range(2)]
        bp_dn = [[[Buf() for _ in range(NFG)] for _ in range(16)] for _ in range(2)]
        bp_qkv = [Buf() for _ in range(48)]
        bp_wo = [Buf() for _ in range(16)]

        def cast(dst, src, b):
            P.op("pool", lambda e: e.dma_start(out=dst, in_=src), writes=[b], dma=True)

        def conv_pool_w():
            for g in range(4):
                v = w_pool[g].rearrange("(kc p) (oc m) -> oc p kc m", p=128, m=128)
                for oc in range(4):
                    cast(s_wpool[g * 4 + oc], v[oc], bp_pool[g * 4 + oc])

        def conv_up(l, fcs):
            v = w_up[l].rearrange("(dc p) (fc m) -> fc p dc m", p=128, m=128)
            for fc in fcs:
                cast(s_wup[l, fc], v[fc], bp_up[l][fc])

        def conv_dn(l, fgs):
            v = w_dn[l].rearrange("(fg j p) (oc m) -> oc fg p j m", j=FGS, p=128, m=128)
            for fg in fgs:
                for oc in range(16):
                    cast(s_wdn[l, oc, fg], v[oc, fg], bp_dn[l][oc][fg])

        def conv_qkv():
            v = w_qkv.rearrange("(dc p) (oc m) -> oc p dc m", p=128, m=128)
            for oc in range(48):
                cast(s_wqkv[oc], v[oc], bp_qkv[oc])

        def conv_wo():
            v = w_o.rearrange("(dc p) (oc m) -> oc p dc m", p=128, m=128)
            for oc in range(16):
                cast(s_wo[oc], v[oc], bp_wo[oc])

        conv_pool_w()
        for fg in range(NFG):
            fcs = list(range(fg * FGS, (fg + 1) * FGS))
            conv_up(0, fcs + [NFC + f for f in fcs])
            conv_dn(0, [fg])
        conv_qkv()

        def load_panel(src, b_src, nk=16):
            s = ws_ring.next()
            P.op("sp", lambda e: e.dma_start(out=wsl[s][:, 0:nk, :], in_=src), reads=[b_src], writes=[bws[s]],
                 dma=True)
            return s

        def norm_stats(c0, W):
            blks = colblocks(W)
            bks = [bank_ring.next() for _ in blks]
            for q in range(4):
                s = r2["sq"].next()
                if q % 2 == 0:
                    P.op("dve", lambda e, s=s, q=q: e.tensor_tensor(out=sq_[s][:, :, 0:W], in0=X[:, 4 * q:4 * q + 4, c0:c0 + W],
                                                                    in1=X[:, 4 * q:4 * q + 4, c0:c0 + W], op=ALU.mult),
                         reads=bX[4 * q:4 * q + 4], writes=[bsq[s]])
                else:
                    P.op("act", lambda e, s=s, q=q: e.activation(sq_[s][:, :, 0:W], X[:, 4 * q:4 * q + 4, c0:c0 + W], AF.Square),
                         reads=bX[4 * q:4 * q + 4], writes=[bsq[s]])

                def mm(e, s=s, q=q):
                    r = None
                    for j in range(4):
                        for (b0, bw), bk in zip(blks, bks):
                            r = e.matmul(bank(bk)[:, 0:bw], ones[:, :], sq_[s][:, j, b0:b0 + bw],
                                         start=(q == 0 and j == 0), stop=(q == 3 and j == 3))
                    return r
                P.op("pe", mm, reads=[bsq[s], bconst], writes=[bbank[k] for k in bks])
            for (b0, bw), bk in zip(blks, bks):
                P.op("act", lambda e, b0=b0, bw=bw, bk=bk: e.activation(rstd[:, b0:b0 + bw], bank(bk)[:, 0:bw], AF.Sqrt,
                                                                        bias=epst[:, 0:1], scale=1.0 / D),
                     reads=[bbank[bk], bconst], writes=[brstd])
            P.op("dve", lambda e: e.reciprocal(rstd[:, 0:W], rstd[:, 0:W]), reads=[brstd], writes=[brstd])

        def norm_apply(c0, W, gcol, out_t, oc0, bout):
            for dc in range(NDC):
                P.op("dve", lambda e, dc=dc: e.scalar_tensor_tensor(
                    out=out_t[:, dc, oc0:oc0 + W], in0=X[:, dc, c0:c0 + W], scalar=pp[:, gcol + dc:gcol + dc + 1],
                    in1=rstd[:, 0:W], op0=ALU.mult, op1=ALU.mult),
                    reads=[bX[dc], brstd, bpp], writes=[bout[dc]])

        def ffn(c0, T, l):
            drain(ffn_gen(c0, T, l))

        def ffn_gen(c0, T, l):
            W = T + 2
            norm_stats(c0, W)
            norm_apply(c0, W, O_FFNN + 16 * l, U, 0, bU)
            P.op("dve", lambda e: e.tensor_copy(Ue[:, :, :], U[:, :, 0:W:W - 1]), reads=bU, writes=[bUe])
            cwc = O_CW + l * 132
            cbc = O_CB + l * 44
            for fg in range(NFG):
                for j in range(FGS):
                    fc = fg * FGS + j
                    sg = load_panel(s_wup[l, fc], bp_up[l][fc])
                    sv = load_panel(s_wup[l, NFC + fc], bp_up[l][NFC + fc])
                    bg, bv, ek = bank_ring.next(), bank_ring.next(), edge_ring.next()

                    eap = edge_ap(ek)

                    def mm_g(e, sg=sg, bg=bg, eap=eap):
                        for dc in range(NDC):
                            e.matmul(bank(bg)[:, 0:T], wsl[sg][:, dc, :], U[:, dc, 1:1 + T], start=(dc == 0), stop=(dc == 15))
                        r = None
                        for dc in range(NDC):
                            r = e.matmul(eap, wsl[sg][:, dc, :], Ue[:, dc, :], start=(dc == 0), stop=(dc == 15))
                        return r
                    if fc == 0:
                        for dc in range(NDC):
                            P.op("pe", lambda e, sg=sg, bg=bg, dc=dc: e.matmul(
                                bank(bg)[:, 0:T], wsl[sg][:, dc, :], U[:, dc, 1:1 + T], start=(dc == 0), stop=(dc == 15)),
                                reads=[bws[sg], bU[dc]], writes=[bbank[bg]])

                        def mm_e(e, sg=sg, eap=eap):
                            r = None
                            for dc in range(NDC):
                                r = e.matmul(eap, wsl[sg][:, dc, :], Ue[:, dc, :], start=(dc == 0), stop=(dc == 15))
                            return r
                        P.op("pe", mm_e, reads=[bws[sg], bUe], writes=[bedge[ek]])
                    else:
                        P.op("pe", mm_g, reads=[bws[sg], bUe] + bU, writes=[bbank[bg], bedge[ek]])

                    def mm_v(e, sv=sv, bv=bv):
                        r = None
                        for dc in range(NDC):
                            r = e.matmul(bank(bv)[:, 0:T], wsl[sv][:, dc, :], U[:, dc, 1:1 + T], start=(dc == 0), stop=(dc == 15))
                        return r
                    P.op("pe", mm_v, reads=[bws[sv]] + bU, writes=[bbank[bv]])
                    g, c, q = r2["gsb"].next(), r2["cb"].next(), r2["ge"].next()

                    def f_copy(e, g=g, bg=bg, eap=eap):
                        e.activation(gsb[g][:, 1:1 + T], bank(bg)[:, 0:T], AF.Identity)
                        return e.activation(gsb[g][:, 0:W:W - 1], eap, AF.Identity)
                    P.op("act", f_copy, reads=[bbank[bg], bedge[ek]], writes=[bgsb[g]])
                    P.op("act", lambda e, c=c, bg=bg, fc=fc: e.activation(
                        cb_[c][:, 0:T], bank(bg)[:, 0:T], AF.Identity, bias=pp[:, cbc + fc:cbc + fc + 1],
                        scale=pp[:, cwc + 44 + fc:cwc + 44 + fc + 1]),
                        reads=[bbank[bg], bpp], writes=[bcb[c]])

                    P.op("dve", lambda e, g=g, c=c, fc=fc: e.scalar_tensor_tensor(
                        out=cb_[c][:, 0:T], in0=gsb[g][:, 0:T], scalar=pp[:, cwc + fc:cwc + fc + 1],
                        in1=cb_[c][:, 0:T], op0=ALU.mult, op1=ALU.add), reads=[bgsb[g], bcb[c], bpp], writes=[bcb[c]])
                    P.op("dve", lambda e, g=g, c=c, fc=fc: e.scalar_tensor_tensor(
                        out=cb_[c][:, 0:T], in0=gsb[g][:, 2:2 + T], scalar=pp[:, cwc + 88 + fc:cwc + 88 + fc + 1],
                        in1=cb_[c][:, 0:T], op0=ALU.mult, op1=ALU.add), reads=[bgsb[g], bcb[c], bpp], writes=[bcb[c]])
                    P.op("act", lambda e, c=c, q=q: e.activation(ge_[q][:, 0:T], cb_[c][:, 0:T], AF.Gelu),
                         reads=[bcb[c]], writes=[bge[q]])
                    P.op("dve", lambda e, q=q, bv=bv, j=j: e.tensor_tensor(out=A[:, j, 0:T], in0=ge_[q][:, 0:T],
                                                                          in1=bank(bv)[:, 0:T], op=ALU.mult),
                         reads=[bge[q], bbank[bv]], writes=[bA[j]])
                    yield
                for oc in range(NDC):
                    sd = load_panel(s_wdn[l, oc, fg], bp_dn[l][oc][fg], nk=FGS)
                    bd = bank_ring.next()

                    def mm_d(e, sd=sd, bd=bd):
                        r = None
                        for j in range(FGS):
                            r = e.matmul(bank(bd)[:, 0:T], wsl[sd][:, j, :], A[:, j, 0:T], start=(j == 0), stop=(j == FGS - 1))
                        return r
                    P.op("pe", mm_d, reads=[bws[sd]] + bA, writes=[bbank[bd]])
                    P.op("dve", lambda e, oc=oc, bd=bd: e.tensor_tensor(
                        out=X[:, oc, c0 + 1:c0 + 1 + T], in0=X[:, oc, c0 + 1:c0 + 1 + T], in1=bank(bd)[:, 0:T], op=ALU.add),
                        reads=[bbank[bd], bX[oc]], writes=[bX[oc]])
                    yield

        with ExitStack() as es1:
            def SB1(name, shape, dt):
                return es1.enter_context(nc.sbuf_tensor(name, shape, dt))
            ya = [SB1("ya%d" % i, [128, 4, MAXW], F32) for i in range(3)]
            bya = [Buf() for _ in range(3)]
            dT = [SB1("dT%d" % i, [128, 4, MAXT + 2], BF16) for i in range(2)]
            bdT = [Buf() for _ in range(2)]
            vld = SB1("vld", [128, MAXW], F32)
            bvld = Buf()
            cnt = [SB1("cnt%d" % i, [128, MAXW], F32) for i in range(2)]
            rcnt = SB1("rcnt", [128, 4, MAXT + 2], F32)
            bcnt2 = [Buf(), Buf()]
            brc = [Buf() for _ in range(4)]
            ptmp = SB1("ptmp", [128, MAXT + 2], F32)
            bptmp = Buf()
            qk_st = [SB1("qk%d" % i, [128, MAXT], BF16) for i in range(2)]
            bqk = [Buf() for _ in range(2)]
            vT = [SB1("vT%d" % i, [128, MAXT], BF16) for i in range(2)]
            bvT = [Buf() for _ in range(2)]
            vtok = [SB1("vtok%d" % i, [128, 4, 512], BF16) for i in range(2)]
            bvtok = [Buf() for _ in range(2)]
            Etmp = [SB1("Etmp%d" % i, [128, 12, 128], F32) for i in range(2)]
            bEtmp = [Buf() for _ in range(2)]
            cmask = SB1("cmask", [128, 128], F32)
            bcm = Buf()

            def build_etables(heads):
                if heads and heads[0] == 0:
                    P.op("sp", lambda e: e.dma_start(out=cmask[:, :], in_=cmin), writes=[bcm], dma=True)
                for h in heads:
                    t = r2["Etmp"].next()
                    for kr in range(2):
                        for qr in range(2):
                            o_first = -6 + kr - qr
                            src = bass.AP(rpbF_h, (h * 15 + o_first + 7) * 64 * 32 + 15, [[31, 64], [2 * 64 * 32, 7], [1, 64]])
                            P.op("sp", lambda e, t=t, kr=kr, qr=qr, src=src: e.dma_start(
                                out=Etmp[t][kr * 64:kr * 64 + 64, 0:7, qr * 64:qr * 64 + 64], in_=src),
                                writes=[bEtmp[t]], dma=True)
                    P.op("act", lambda e, t=t: e.activation(Etmp[t][:, 0:7, :], Etmp[t][:, 0:7, :], AF.Exp),
                         reads=[bEtmp[t]], writes=[bEtmp[t]])

                    P.op("dve", lambda e, t=t: e.tensor_tensor(out=Etmp[t][:, 0:7, :], in0=Etmp[t][:, 0:7, :],
                                                               in1=cmask[:, :].unsqueeze(1).to_broadcast([128, 7, 128]), op=ALU.mult),
                         reads=[bEtmp[t], bcm], writes=[bEtmp[t]])
                    P.op("dve", lambda e, t=t: e.tensor_copy(Etmp[t][:, 7:12, :], Etmp[t][:, 1:6, :]),
                         reads=[bEtmp[t]], writes=[bEtmp[t]])

                    def f_e(e, t=t):
                        e.memset(Etmp[t][0:64, 7, 64:128], 0.0)
                        e.memset(Etmp[t][64:128, 11, :], 0.0)
                        return e.memset(Etmp[t][0:64, 11, 0:64], 0.0)
                    P.op("dve", f_e, reads=[bEtmp[t]], writes=[bEtmp[t]])
                    P.op("sp", lambda e, t=t, h=h: e.dma_start(out=s_E[h].rearrange("p (i q) -> p i q", q=128), in_=Etmp[t][:, :, :]),
                         reads=[bEtmp[t]], dma=True)


            etab_done = [0]
            rest = []
            rest.append(conv_wo)
            for fg in range(NFG):
                fcs = list(range(fg * FGS, (fg + 1) * FGS))
                rest.append(lambda fcs=fcs: conv_up(1, fcs + [NFC + f for f in fcs]))
                rest.append(lambda fg=fg: conv_dn(1, [fg]))

            prefetched = {}
            for t_, b_ in ((cnt[0], bcnt2[0]), (cnt[1], bcnt2[1]), (ya[0], bya[0]), (ya[1], bya[1]), (ya[2], bya[2])):
                P.op("pool", lambda e, t_=t_: e.memset(t_[tuple(slice(None) for _ in t_.shape)], 0.0), writes=[b_])
            def tile_geom(ti):
                lr0, nr = P1_TILES[ti]
                T = nr * GW
                return T, T + 2 * XH, lr0 * GW, P1_OFF[ti], T + 2, XH - 1

            def gen_in(ti):
                T, W, lt0, xoff, PW, pc0 = tile_geom(ti)
                P.op("sp", lambda e, xoff=xoff, W=W: e.dma_start(out=vld[:, 0:W], in_=vldin[:, xoff:xoff + W]),
                     writes=[bvld], dma=True)
                for bi, (b0, nb) in enumerate(inblocks(W)):
                    if (ti, bi) in prefetched:
                        ts = prefetched[(ti, bi)]
                    else:
                        ts = r2["tin"].next()
                        P.op("sp", lambda e, ts=ts, b0=b0, nb=nb, xoff=xoff: e.dma_start(
                            out=tin[ts][0:nb, :], in_=xin[xoff + b0:xoff + b0 + nb, :]), writes=[btin[ts]], dma=True)
                    for q in range(4):
                        bk = bank_ring.next()

                        def f_tr(e, ts=ts, nb=nb, q=q, bk=bk):
                            r = None
                            for j in range(4):
                                dc = 4 * q + j
                                r = e.transpose(bank(bk)[:, j * 128:j * 128 + nb], tin[ts][0:nb, dc * 128:(dc + 1) * 128],
                                                ident[0:nb, 0:nb])
                            return r
                        P.op("pe", f_tr, reads=[btin[ts], bconst], writes=[bbank[bk]])
                        src = bank(bk).rearrange("p (j t) -> p j t", t=128)[:, :, 0:nb]
                        if evac_rr.next() == "act":
                            P.op("act", lambda e, q=q, b0=b0, nb=nb, src=src: e.activation(
                                X[:, 4 * q:4 * q + 4, b0:b0 + nb], src, AF.Identity),
                                reads=[bbank[bk]], writes=bX[4 * q:4 * q + 4])
                        else:
                            P.op("dve", lambda e, q=q, b0=b0, nb=nb, src=src: e.tensor_copy(
                                X[:, 4 * q:4 * q + 4, b0:b0 + nb], src),
                                reads=[bbank[bk]], writes=bX[4 * q:4 * q + 4])
                yield
                norm_stats(0, W)

                src_t, src_b = vld, bvld
                for gi in range(4):
                    dst_t, dst_b = cnt[gi % 2], bcnt2[gi % 2]
                    sh = (1 << gi) >> 1
                    if gi == 0:
                        P.op("dve", lambda e, dst_t=dst_t, src_t=src_t, W=W: e.tensor_tensor(
                            out=dst_t[:, 1:W], in0=src_t[:, 0:W - 1], in1=src_t[:, 1:W], op=ALU.add),
                            reads=[src_b], writes=[dst_b])
                    else:
                        P.op("dve", lambda e, dst_t=dst_t, src_t=src_t, W=W, sh=sh: e.tensor_tensor(
                            out=dst_t[:, sh:W - sh], in0=src_t[:, 0:W - 2 * sh], in1=src_t[:, 2 * sh:W], op=ALU.add),
                            reads=[src_b], writes=[dst_b])
                    P.op("dve", lambda e, gi=gi, dst_t=dst_t, PW=PW, pc0=pc0: e.tensor_scalar(
                        out=rcnt[:, gi, 0:PW], in0=dst_t[:, pc0:pc0 + PW], scalar1=1.0, scalar2=None, op0=ALU.max),
                        reads=[dst_b], writes=[brc[gi]])
                    P.op("dve", lambda e, gi=gi, PW=PW: e.reciprocal(rcnt[:, gi, 0:PW], rcnt[:, gi, 0:PW]),
                         reads=[brc[gi]], writes=[brc[gi]])
                    src_t, src_b = dst_t, dst_b
                yield
                for g in range(4):
                    y0, y1, y2 = ya[0], ya[1], ya[2]

                    P.op("dve", lambda e, g=g, W=W: e.tensor_tensor(
                        out=y0[:, :, 0:W], in0=X[:, 4 * g:4 * g + 4, 0:W],
                        in1=rstd[:, 0:W].unsqueeze(1).to_broadcast([128, 4, W]), op=ALU.mult),
                        reads=bX[4 * g:4 * g + 4] + [brstd], writes=[bya[0]])
                    P.op("dve", lambda e, W=W: e.tensor_tensor(out=y1[:, :, 1:W], in0=y0[:, :, 0:W - 1], in1=y0[:, :, 1:W], op=ALU.add),
                         reads=[bya[0]], writes=[bya[1]])
                    si, di = 1, 2
                    for gi in range(1, g + 1):
                        sh = 1 << (gi - 1)
                        P.op("dve", lambda e, si=si, di=di, sh=sh, W=W: e.tensor_tensor(
                            out=ya[di][:, :, sh:W - sh], in0=ya[si][:, :, 0:W - 2 * sh], in1=ya[si][:, :, 2 * sh:W], op=ALU.add),
                            reads=[bya[si]], writes=[bya[di]])
                        si, di = di, si
                    res_buf, bres = ya[di], bya[di]
                    P.op("dve", lambda e, si=si, di=di, g=g, PW=PW, pc0=pc0: e.tensor_tensor(
                        out=ya[di][:, :, 0:PW], in0=ya[si][:, :, pc0:pc0 + PW],
                        in1=rcnt[:, g, 0:PW].unsqueeze(1).to_broadcast([128, 4, PW]), op=ALU.mult),
                        reads=[bya[si], brc[g]], writes=[bya[di]])
                    P.op("dve", lambda e, di=di, PW=PW, pc0=pc0: e.tensor_tensor(
                        out=ya[di][:, :, 0:PW], in0=ya[di][:, :, 0:PW], in1=y0[:, :, pc0:pc0 + PW], op=ALU.subtract),
                        reads=[bya[di], bya[0]], writes=[bya[di]])
                    dsl = r2["dT"].next()
                    for j in range(4):
                        P.op("dve", lambda e, j=j, g=g, dsl=dsl, res_buf=res_buf, PW=PW: e.tensor_scalar(
                            out=dT[dsl][:, j, 0:PW], in0=res_buf[:, j, 0:PW], scalar1=pp[:, O_MIXN + 4 * g + j:O_MIXN + 4 * g + j + 1],
                            scalar2=None, op0=ALU.mult), reads=[bres, bpp], writes=[bdT[dsl]])
                    yield
                    for oc in range(4):
                        s = load_panel(s_wpool[g * 4 + oc], bp_pool[g * 4 + oc], nk=4)
                        blks = colblocks(PW)
                        bks = [bank_ring.next() for _ in blks]

                        def mm_p(e, s=s, dsl=dsl, blks=blks, bks=bks):
                            r = None
                            for kc in range(4):
                                for (b0, bw), bk in zip(blks, bks):
                                    r = e.matmul(bank(bk)[:, 0:bw], wsl[s][:, kc, :], dT[dsl][:, kc, b0:b0 + bw],
                                                 start=(kc == 0), stop=(kc == 3))
                            return r
                        P.op("pe", mm_p, reads=[bws[s], bdT[dsl]], writes=[bbank[k] for k in bks])
                        dc = 4 * g + oc
                        for (b0, bw), bk in zip(blks, bks):
                            P.op("act", lambda e, b0=b0, bw=bw, bk=bk, dc=dc: e.activation(
                                ptmp[:, b0:b0 + bw], bank(bk)[:, 0:bw], AF.Identity, bias=pp[:, O_PBS + dc:O_PBS + dc + 1],
                                scale=pp[:, O_PS + dc:O_PS + dc + 1]), reads=[bbank[bk], bpp], writes=[bptmp])
                        P.op("dve", lambda e, dc=dc, PW=PW, pc0=pc0: e.tensor_tensor(
                            out=X[:, dc, pc0:pc0 + PW], in0=X[:, dc, pc0:pc0 + PW], in1=ptmp[:, 0:PW], op=ALU.add),
                            reads=[bptmp, bX[dc]], writes=[bX[dc]])
                    yield

            def stage_mid(ti):
                T, W, lt0, xoff, PW, pc0 = tile_geom(ti)

                def f_ez(e, pc0=pc0, PW=PW):
                    e.tensor_scalar(out=X[:, :, pc0:pc0 + 1], in0=X[:, :, pc0:pc0 + 1], scalar1=vld[:, pc0:pc0 + 1],
                                    scalar2=None, op0=ALU.mult)
                    return e.tensor_scalar(out=X[:, :, pc0 + PW - 1:pc0 + PW], in0=X[:, :, pc0 + PW - 1:pc0 + PW],
                                           scalar1=vld[:, pc0 + PW - 1:pc0 + PW], scalar2=None, op0=ALU.mult)
                P.op("dve", f_ez, reads=bX + [bvld], writes=bX)
                ffn(pc0, T, 0)
                P.op("act", lambda e, lt0=lt0, T=T: e.dma_start(
                    out=s_x1T.rearrange("(dc p) t -> p dc t", p=128)[:, :, lt0:lt0 + T], in_=X[:, :, XH:XH + T]),
                    reads=bX, dma=True)
                if p1_sel is None and ti + 1 < len(P1_TILES):
                    nW = P1_TILES[ti + 1][1] * GW + 2 * XH
                    nxo = P1_OFF[ti + 1]
                    for bi, (b0, nb) in enumerate(inblocks(nW)[:2]):
                        ts = r2["tin"].next()
                        prefetched[(ti + 1, bi)] = ts
                        P.op("sp", lambda e, ts=ts, b0=b0, nb=nb, nxo=nxo: e.dma_start(
                            out=tin[ts][0:nb, :], in_=xin[nxo + b0:nxo + b0 + nb, :]), writes=[btin[ts]], dma=True)
                norm_stats(XH, T)
                norm_apply(XH, T, O_MIXN + 16, U, 0, bU)

            def gen_qkv(ti):
                T, W, lt0, xoff, PW, pc0 = tile_geom(ti)
                vk = None
                for oc in range(48):
                    s = load_panel(s_wqkv[oc], bp_qkv[oc])
                    bk = bank_ring.next()

                    def mm_q(e, s=s, bk=bk, T=T):
                        r = None
                        for dc in range(NDC):
                            r = e.matmul(bank(bk)[:, 0:T], wsl[s][:, dc, :], U[:, dc, 0:T], start=(dc == 0), stop=(dc == 15))
                        return r
                    P.op("pe", mm_q, reads=[bws[s]] + bU, writes=[bbank[bk]])
                    if oc < 32:
                        st = r2["qk"].next()
                        if oc < 16:
                            bias, scl, dst = pp[:, O_BQS + oc:O_BQS + oc + 1], float(DH ** -0.5), s_QT
                        else:
                            bias, scl, dst = pp[:, O_BK + oc - 16:O_BK + oc - 15], 1.0, s_KT
                        P.op("act", lambda e, st=st, bk=bk, bias=bias, scl=scl, T=T: e.activation(
                            qk_st[st][:, 0:T], bank(bk)[:, 0:T], AF.Identity, bias=bias, scale=scl),
                            reads=[bbank[bk], bpp], writes=[bqk[st]])
                        h = oc % 16
                        P.op("act", lambda e, st=st, dst=dst, h=h, lt0=lt0, T=T: e.dma_start(
                            out=dst[h * 128:(h + 1) * 128, lt0:lt0 + T], in_=qk_st[st][:, 0:T]), reads=[bqk[st]], dma=True)
                    else:
                        h = oc - 32
                        vs = r2["vT"].next()
                        P.op("act", lambda e, vs=vs, bk=bk, h=h, T=T: e.activation(
                            vT[vs][:, 0:T], bank(bk)[:, 0:T], AF.Identity, bias=pp[:, O_BV + h:O_BV + h + 1], scale=1.0),
                            reads=[bbank[bk], bpp], writes=[bvT[vs]])
                        nblk = T // 128
                        bk2 = bank_ring.next()
                        pbf = bank(bk2).bitcast(BF16)

                        def f_vtr(e, vs=vs, pbf=pbf, nblk=nblk):
                            r = None
                            for b in range(nblk):
                                r = e.transpose(pbf[:, b * 128:(b + 1) * 128], vT[vs][:, b * 128:(b + 1) * 128], identb[:, :])
                            return r
                        P.op("pe", f_vtr, reads=[bvT[vs], bconst], writes=[bbank[bk2]])
                        if h % 4 == 0:
                            vk = r2["vtok"].next()
                        hq = h % 4
                        P.op("dve", lambda e, vk=vk, hq=hq, pbf=pbf, nblk=nblk: e.tensor_copy(
                            vtok[vk][:, 0:nblk, hq * 128:(hq + 1) * 128],
                            pbf[:, 0:nblk * 128].rearrange("p (b c) -> p b c", c=128)),
                            reads=[bbank[bk2]], writes=[bvtok[vk]])
                        if hq == 3:
                            hg = h // 4
                            P.op("act", lambda e, vk=vk, hg=hg, lt0=lt0, T=T, nblk=nblk: e.dma_start(
                                out=s_V[lt0:lt0 + T, hg * 512:(hg + 1) * 512].rearrange("(b p) c -> p b c", p=128),
                                in_=vtok[vk][:, 0:nblk, :]), reads=[bvtok[vk]], dma=True)
                    yield

            tiles = [ti for ti in range(len(P1_TILES)) if p1_sel is None or ti in p1_sel]
            drain(gen_in(tiles[0]))
            stage_mid(tiles[0])
            for k, ti in enumerate(tiles):
                gq = gen_qkv(ti)
                if k + 1 < len(tiles):
                    ga = gen_in(tiles[k + 1])
                    for step in (("q", 4), ("a", 1), ("q", 4), ("a", 1), ("q", 4), ("a", 1), ("q", 6), ("a", 2),
                                 ("q", 6), ("a", 2), ("q", 7), ("a", 2), ("q", 9), ("a", 1)):
                        drain(gq if step[0] == "q" else ga, step[1])
                    drain(gq)
                    drain(ga)
                    stage_mid(tiles[k + 1])
                else:
                    drain(gq)
                if rest:
                    rest.pop(0)()
                if k >= 1 and etab_done[0] < NH:
                    build_etables(list(range(etab_done[0], etab_done[0] + 2)))
                    etab_done[0] += 2
            while rest:
                rest.pop(0)()
            if etab_done[0] < NH:
                build_etables(list(range(etab_done[0], NH)))
            P.barrier()
        with ExitStack() as es2:
            def SB2(name, shape, dt):
                return es2.enter_context(nc.sbuf_tensor(name, shape, dt))
            Qb = SB2("Qb", [128, NH, MAXT + 2], BF16)
            bQb = Buf()
            Kb = [SB2("Kb%d" % i, [128, 2, 1152], BF16) for i in range(2)]
            bKb = [Buf() for _ in range(2)]
            Vb = [SB2("Vb%d" % i, [128, 9, 256], BF16) for i in range(2)]
            bVb = [Buf() for _ in range(2)]
            Eb = [SB2("Eb%d" % i, [128, 12, 128], F32) for i in range(2)]
            bEb = [Buf() for _ in range(2)]
            pexp = [SB2("pexp%d" % i, [128, 6, 128], F32) for i in range(2)]
            bpexp = [Buf() for _ in range(2)]
            PT = [SB2("PT%d" % i, [128, 6, 128], BF16) for i in range(2)]
            bPT = [Buf() for _ in range(2)]
            rden = [SB2("rden%d" % i, [128, 128], F32) for i in range(2)]
            brden = [Buf() for _ in range(2)]
            mk = SB2("mk", [128, 12, 6, 2], F32)
            ev = SB2("ev", [128, 2 * N_P2], F32)
            bmk = Buf()
            P.op("sp", lambda e: e.dma_start(out=mk[:, :, :, :], in_=mkin.rearrange("p (a b c) -> p a b c", b=6, c=2)),
                 writes=[bmk], dma=True)
            P.op("sp", lambda e: e.dma_start(out=ev[:, :], in_=evin), writes=[bmk], dma=True)
            OT = SB2("OT", [128, NH, MAXT + 2], BF16)
            bOT = [Buf() for _ in range(NH)]
            bod = [Buf(), Buf()]
            bank_ring.items = [4, 5, 6]
            out_stores = []

            pendB = [None]

            def attn(h, hh, ks, vs, es_, chunks, e0, qa, nq, q_ap, o_ap, mk_ap):
                n = len(chunks)
                sp_ = 0
                S = psT[sp_]

                def mm_s(e):
                    r = None
                    for t, j in enumerate(chunks):
                        r = e.matmul(S[:, t * 128:t * 128 + nq], Kb[ks][:, hh, j * 128:(j + 1) * 128], q_ap, start=True, stop=True)
                    return r
                P.op("pe", mm_s, reads=[bKb[ks], bQb], writes=[bbank[2 * sp_], bbank[2 * sp_ + 1]])
                px, pt, rd = r2["pexp"].next(), r2["PT"].next(), r2["rden"].next()

                def f_exp(e):
                    return e.activation(pexp[px][:, 0:n, 0:nq], S[:, 0:n * 128].rearrange("p (t q) -> p t q", q=128)[:, :, 0:nq], AF.Exp)
                P.op("act", f_exp, reads=[bbank[2 * sp_], bbank[2 * sp_ + 1]], writes=[bpexp[px]])

                P.op("dve", lambda e: e.tensor_tensor(out=PT[pt][:, 0:n, 0:nq], in0=pexp[px][:, 0:n, 0:nq],
                                                      in1=Eb[es_][:, e0:e0 + n, qa:qa + nq], op=ALU.mult),
                     reads=[bpexp[px], bEb[es_]], writes=[bPT[pt]])
                if mk_ap is not None:
                    P.op("dve", lambda e: e.tensor_tensor(out=PT[pt][:, 0:n, :].rearrange("p t (r c) -> p t r c", r=2),
                                                          in0=PT[pt][:, 0:n, :].rearrange("p t (r c) -> p t r c", r=2),
                                                          in1=mk_ap.unsqueeze(3).to_broadcast([128, n, 2, 64]), op=ALU.mult),
                         reads=[bPT[pt], bmk], writes=[bPT[pt]])

                def partB():
                    ob = 2 + r2["od"].next()
                    ocol = 0
                    bo_ = bbank[ob]

                    def mm_o(e):
                        for t, j in enumerate(chunks):
                            e.matmul(bank(ob)[:, ocol:ocol + nq], Vb[vs][:, j, hh * 128:(hh + 1) * 128], PT[pt][:, t, 0:nq],
                                     start=(t == 0), stop=(t == n - 1))
                        r = None
                        for t in range(n):
                            r = e.matmul(bank(ob)[:, ocol + 128:ocol + 128 + nq], ones[:, :], PT[pt][:, t, 0:nq], start=(t == 0), stop=(t == n - 1))
                        return r
                    P.op("pe", mm_o, reads=[bVb[vs], bPT[pt], bconst], writes=[bo_])
                    P.op("act", lambda e: e.activation(rden[rd][:, 0:nq], bank(ob)[:, ocol + 128:ocol + 128 + nq], AF.Ln),
                         reads=[bo_], writes=[brden[rd]])
                    P.op("act", lambda e: e.activation(rden[rd][:, 0:nq], rden[rd][:, 0:nq], AF.Exp, scale=-1.0),
                         reads=[brden[rd]], writes=[brden[rd]])
                    P.op("dve", lambda e: e.tensor_tensor(out=o_ap, in0=bank(ob)[:, ocol:ocol + nq], in1=rden[rd][:, 0:nq], op=ALU.mult),
                         reads=[bo_, brden[rd]], writes=[bOT[h]])
                if pendB[0] is not None:
                    pendB[0]()
                pendB[0] = partB

            def attn_flush():
                if pendB[0] is not None:
                    pendB[0]()
                    pendB[0] = None

            T = MAXT

            def gen_attn(i):
                q0 = HALO_TOP + 8 * i
                tq0 = q0 * GW
                wt0 = (q0 - 6) * GW
                P.op("sp", lambda e, tq0=tq0: e.dma_start(
                    out=Qb[:, :, 0:T + 2], in_=s_QT.rearrange("(h p) t -> p h t", p=128)[:, :, tq0 - 1:tq0 + T + 1]),
                    writes=[bQb], dma=True)
                gmod = (8 * i) % 32
                for hg in range(8):
                    ks, vs = r2["Kb"].next(), r2["Vb"].next()
                    P.op("sp", lambda e, ks=ks, hg=hg, wt0=wt0: e.dma_start(
                        out=Kb[ks][:, :, :], in_=s_KT[hg * 256:(hg + 1) * 256, wt0:wt0 + 1152].rearrange("(h p) t -> p h t", p=128)),
                        writes=[bKb[ks]], dma=True)
                    P.op("sp", lambda e, vs=vs, hg=hg, wt0=wt0: e.dma_start(
                        out=Vb[vs][:, :, :], in_=s_V[wt0:wt0 + 1152, hg * 256:(hg + 1) * 256].rearrange("(j p) c -> p j c", p=128)),
                        writes=[bVb[vs]], dma=True)
                    for hh in range(2):
                        h = 2 * hg + hh
                        es_ = r2["Eb"].next()
                        P.op("sp", lambda e, es_=es_, h=h: e.dma_start(
                            out=Eb[es_][:, :, :], in_=s_E[h].rearrange("p (i q) -> p i q", q=128)), writes=[bEb[es_]], dma=True)
                        attn(h, hh, ks, vs, es_, [0, 1, 2, 3, 4], 7, 127, 1, Qb[:, h, 0:1], OT[:, h, 0:1], None)
                        yield
                        for k in range(4):
                            mk_ap = None
                            if gmod == 0 and k < 2:
                                chunks, e0 = list(range(k + 1, k + 7)), 1
                                mk_ap = mk[:, (i // 4) * 4 + k, :, :]
                            elif gmod == 24 and k >= 2:
                                chunks, e0 = list(range(k, k + 6)), 0
                                mk_ap = mk[:, (i // 4) * 4 + k, :, :]
                            else:
                                chunks, e0 = list(range(k + 1, k + 6)), 7
                            attn(h, hh, ks, vs, es_, chunks, e0, 0, 128, Qb[:, h, 1 + 128 * k:1 + 128 * k + 128],
                                 OT[:, h, 1 + 128 * k:1 + 128 * k + 128], mk_ap)
                            yield
                        attn(h, hh, ks, vs, es_, [5, 6, 7, 8], 7, 0, 1, Qb[:, h, T + 1:T + 2], OT[:, h, T + 1:T + 2], None)
                        yield
                attn_flush()

            def wo_stage(i):
                tq0 = (HALO_TOP + 8 * i) * GW
                P.op("sp", lambda e, tq0=tq0: e.dma_start(
                    out=X[:, :, 0:T + 2], in_=s_x1T.rearrange("(dc p) t -> p dc t", p=128)[:, :, tq0 - 1:tq0 + T + 1]),
                    writes=bX, dma=True)
                for oc in range(NDC):
                    s = load_panel(s_wo[oc], bp_wo[oc])
                    bk, ek = bank_ring.next(), edge_ring.next()

                    eap = edge_ap(ek)

                    def mm_wo(e, s=s, bk=bk, eap=eap):
                        for hc in range(NH):
                            e.matmul(bank(bk)[:, 0:T], wsl[s][:, hc, :], OT[:, hc, 1:1 + T], start=(hc == 0), stop=(hc == 15))
                        r = None
                        for hc in range(NH):
                            r = e.matmul(eap, wsl[s][:, hc, :], OT[:, hc, 0:T + 2:T + 1], start=(hc == 0), stop=(hc == 15))
                        return r
                    P.op("pe", mm_wo, reads=[bws[s]] + bOT, writes=[bbank[bk], bedge[ek]])

                    def f_wo(e, oc=oc, bk=bk, eap=eap):
                        e.tensor_tensor(out=X[:, oc, 1:1 + T], in0=X[:, oc, 1:1 + T], in1=bank(bk)[:, 0:T], op=ALU.add)
                        return e.tensor_tensor(out=X[:, oc, 0:T + 2:T + 1], in0=X[:, oc, 0:T + 2:T + 1], in1=eap, op=ALU.add)
                    P.op("dve", f_wo, reads=[bbank[bk], bedge[ek], bX[oc]], writes=[bX[oc]])

                def f_ez2(e, i=i):
                    e.tensor_scalar(out=X[:, :, 0:1], in0=X[:, :, 0:1], scalar1=ev[:, 2 * i:2 * i + 1], scalar2=None, op0=ALU.mult)
                    return e.tensor_scalar(out=X[:, :, T + 1:T + 2], in0=X[:, :, T + 1:T + 2], scalar1=ev[:, 2 * i + 1:2 * i + 2],
                                           scalar2=None, op0=ALU.mult)
                P.op("dve", f_ez2, reads=bX + [bmk], writes=bX)
                if debug:
                    out_stores.append(P.op("pool", lambda e, i=i: e.dma_start(out=dbg_x[i], in_=X[:, :, 0:T + 2]), reads=bX, dma=True))
                    out_stores.append(P.op("pool", lambda e, i=i: e.dma_start(out=dbg_o[i], in_=OT[:, :, 0:T + 2]), reads=bOT, dma=True))

            def tail_stage(i):
                norm_stats(1, T)
                norm_apply(1, T, O_FIN, X, 1, bX)
                for tb in range(T // 128):
                    ts = r2["tin"].next()
                    for q in range(4):
                        bk = bank_ring.next()

                        def f_tro(e, tb=tb, q=q, bk=bk):
                            r = None
                            for j in range(4):
                                dc = 4 * q + j
                                r = e.transpose(bank(bk)[:, j * 128:(j + 1) * 128], X[:, dc, 1 + tb * 128:1 + (tb + 1) * 128], ident[:, :])
                            return r
                        P.op("pe", f_tro, reads=bX[4 * q:4 * q + 4] + [bconst], writes=[bbank[bk]])
                        if evac_rr.next() == "act":
                            P.op("act", lambda e, ts=ts, q=q, bk=bk: e.activation(tin[ts][:, q * 512:(q + 1) * 512], bank(bk), AF.Identity),
                                 reads=[bbank[bk]], writes=[btin[ts]])
                        else:
                            P.op("dve", lambda e, ts=ts, q=q, bk=bk: e.tensor_copy(tin[ts][:, q * 512:(q + 1) * 512], bank(bk)),
                                 reads=[bbank[bk]], writes=[btin[ts]])
                    r0 = i * 512 + tb * 128
                    out_stores.append(P.op("pool", lambda e, ts=ts, r0=r0: e.dma_start(out=yout[r0:r0 + 128, :], in_=tin[ts][:, :]),
                                           reads=[btin[ts]], dma=True))
            tiles2 = [i for i in range(N_P2) if p2_sel is None or i in p2_sel]
            drain(gen_attn(tiles2[0]))
            for k, i in enumerate(tiles2):
                wo_stage(i)
                gf = ffn_gen(0, T, 1)
                if k + 1 < len(tiles2):
                    ga = gen_attn(tiles2[k + 1])
                    fa, aa = True, True
                    while fa or aa:
                        if fa:
                            fa = drain(gf, 1)
                        if aa:
                            aa = drain(ga, 1)
                else:
                    drain(gf)
                tail_stage(i)
            P.emit(final_wait_ops=out_stores)
    return nc


def _seq_of_row(g):
    for (a, b) in SEQS:
        if a <= g < b:
            return (a, b)
    return None


def _prep_core(c, Xall, pp, consts, weights):
    base_row = ROWS_CORE * c - HALO_TOP
    xin = np.zeros((N_EXT, D), np.float32)
    vld = np.zeros((N_EXT,), np.float32)
    for ti, (lr0, nr) in enumerate(P1_TILES):
        g0 = base_row + lr0
        sq = _seq_of_row(g0)
        if sq is None:
            continue
        t_lo, t_hi = sq[0] * GW, sq[1] * GW
        e0 = g0 * GW - XH
        n = nr * GW + 2 * XH
        a = max(e0, t_lo)
        b = min(e0 + n, t_hi)
        if b > a:
            xin[P1_OFF[ti] + (a - e0):P1_OFF[ti] + (b - e0)] = Xall[a:b]
            vld[P1_OFF[ti] + (a - e0):P1_OFF[ti] + (b - e0)] = 1.0
    ev = np.zeros((2 * N_P2,), np.float32)
    mk = np.zeros((128, 12, 6, 2), np.float32)
    for i in range(N_P2):
        gq0 = ROWS_CORE * c + 8 * i
        sq = _seq_of_row(gq0)
        ev[2 * i] = 1.0 if gq0 - 1 >= sq[0] else 0.0
        ev[2 * i + 1] = 1.0 if gq0 + 8 < sq[1] else 0.0
        gmod = (8 * i) % 32
        for k in range(4):
            if gmod == 0 and k < 2:
                ws = gq0 + 2 * k - 4
            elif gmod == 24 and k >= 2:
                ws = gq0 + 2 * k - 6
            else:
                continue
            pe_i = (i // 4) * 4 + k
            for qr in range(2):
                qg = gq0 + 2 * k + qr
                s0, s1 = _seq_of_row(qg)
                rs = min(max(qg - 4, s0), s1 - 8)
                for n_ in range(12):
                    kg = ws + n_
                    m = 1.0 if rs <= kg < rs + 8 else 0.0
                    j, kr = n_ // 2, n_ % 2
                    mk[kr * 64:(kr + 1) * 64, pe_i, j, qr] = m
    d = {
        "xin": xin,
        "vldin": np.ascontiguousarray(np.broadcast_to(vld[None, :], (128, N_EXT))),
        "evin": np.ascontiguousarray(np.broadcast_to(ev[None, :], (128, 2 * N_P2))),
        "mkin": np.ascontiguousarray(mk.reshape(128, 144)),
        "ppin": pp,
    }
    d.update(consts)
    d.update(weights)
    return d


def _fm(v):
    v = np.asarray(v, np.float32)
    return np.ascontiguousarray(v.reshape(-1, 128).T)


_NC_CACHE = {}


def kernel(x_prompt, x_sample, mix_norm, pool_w, pool_b, pool_scale, attn_w_qkv, attn_b_qkv, attn_rpb, attn_w_o,
           ffn_norm, ffn_w_up, ffn_conv_w, ffn_conv_b, ffn_w_down, final_norm, _debug=False, _p1=None, _p2=None):
    f32 = lambda a: np.ascontiguousarray(np.asarray(a, dtype=np.float32))
    x_prompt, x_sample = f32(x_prompt), f32(x_sample)
    Xall = np.concatenate([x_prompt.reshape(-1, D), x_sample.reshape(-1, D)], axis=0)
    pp = np.zeros((128, NPP), np.float32)
    mix_norm, ffn_norm, final_norm = f32(mix_norm), f32(ffn_norm), f32(final_norm)
    pp[:, O_MIXN:O_MIXN + 16] = _fm(mix_norm[0])
    pp[:, O_MIXN + 16:O_MIXN + 32] = _fm(mix_norm[1])
    pp[:, O_FFNN:O_FFNN + 16] = _fm(ffn_norm[0])
    pp[:, O_FFNN + 16:O_FFNN + 32] = _fm(ffn_norm[1])
    pp[:, O_FIN:O_FIN + 16] = _fm(final_norm)
    pp[:, O_PB:O_PB + 16] = _fm(f32(pool_b).reshape(-1))
    pp[:, O_PS:O_PS + 16] = _fm(f32(pool_scale).reshape(-1))
    cw, cb = f32(ffn_conv_w), f32(ffn_conv_b)
    for l in range(2):
        for k in range(3):
            pp[:, O_CW + l * 132 + k * 44:O_CW + l * 132 + (k + 1) * 44] = _fm(cw[l, k])
        pp[:, O_CB + l * 44:O_CB + (l + 1) * 44] = _fm(cb[l])
    bqkv = f32(attn_b_qkv).reshape(-1)
    pp[:, O_BQ:O_BQ + 16] = _fm(bqkv[0:D])
    pp[:, O_BK:O_BK + 16] = _fm(bqkv[D:2 * D])
    pp[:, O_BV:O_BV + 16] = _fm(bqkv[2 * D:3 * D])
    rpb = f32(attn_rpb).reshape(NH, 15, 31)
    rf = np.zeros((NH, 15, 64, 32), np.float32)
    rf[:, :, :, 0:31] = rpb[:, :, None, ::-1]
    cq = np.arange(GW)
    cs = np.clip(cq - 8, 0, GW - 16)
    cm = ((cq[:, None] >= cs[None, :]) & (cq[:, None] < cs[None, :] + 16)).astype(np.float32)
    consts = {"rpbF": rf.reshape(-1), "cmin": np.ascontiguousarray(np.tile(cm, (2, 2))),
              "idin": np.eye(128, dtype=np.float32)}
    weights = {"w_pool": f32(pool_w).reshape(4, 512, 512), "w_qkv": f32(attn_w_qkv).reshape(D, 3 * D),
               "w_o": f32(attn_w_o).reshape(D, D), "w_up": f32(ffn_w_up), "w_dn": f32(ffn_w_down)}
    in_maps = [_prep_core(c, Xall, pp, consts, weights) for c in range(N_CORES)]
    key = (bool(_debug), str(_p1), str(_p2))
    if key not in _NC_CACHE:
        _NC_CACHE[key] = build_program(debug=_debug, p1_sel=_p1, p2_sel=_p2)
    nc = _NC_CACHE[key]
    res = run_bass_kernel_spmd(nc, in_maps, core_ids=list(range(N_CORES)))
    if _debug:
        return res
    Y = np.concatenate([np.asarray(r["yout"], dtype=np.float32) for r in res.results], axis=0)
    y_prompt = Y[0:16384].reshape(1, 16384, D)
    y_sample = Y[16384:].reshape(4, 8192, D)
    return (np.ascontiguousarray(y_prompt), np.ascontiguousarray(y_sample))
```
